# Optimizing a Trainium2 kernel written in Bass

```python
import math
import jax, jax.numpy as jnp
from jax import lax
import numpy as np

D_MODEL = 1024
BATCH = 8
SEQ = 4096
DEPTH = 1
DEC_BATCH = 4
DEC_SEQ = 8192
PAST_LEN = 128

N_MEM = 256
CONV_CH = 512
CONV_WIDTH = 31
DIFF_HEADS = 4
DIFF_DQ = 64
DIFF_DV = 2 * DIFF_DQ
DIFF_WIDTH = DIFF_HEADS * DIFF_DV
MIX_WIDTH = CONV_CH + DIFF_WIDTH
IN_COLS = 2 * CONV_CH + 3 * DIFF_WIDTH
MEM_HEADS = 4
MEM_HD = D_MODEL // MEM_HEADS
D_FF = 2816
FFN_CONV_WIDTH = 3
ROPE_THETA = 10000.0
Q_BLOCK = 128
EPS = 1e-6

kernel_name = "hybrid_conformer_diffattn_encoder"


def rmsnorm(x, g):
    xf = x.astype(jnp.float32)
    y = xf * lax.rsqrt(jnp.mean(xf * xf, axis=-1, keepdims=True) + EPS)
    return (y * g.astype(jnp.float32)).astype(x.dtype)


def rope(x, pos):
    d = x.shape[-1]
    inv = ROPE_THETA ** (-jnp.arange(0, d, 2, dtype=jnp.float32) / d)
    ang = pos[:, None] * inv[None, :]
    c = jnp.cos(ang)[None, :, None, :]
    s = jnp.sin(ang)[None, :, None, :]
    xf = x.astype(jnp.float32)
    x1, x2 = xf[..., : d // 2], xf[..., d // 2:]
    out = jnp.concatenate([x1 * c - x2 * s, x2 * c + x1 * s], axis=-1)
    return out.astype(x.dtype)


def dwconv(x, w, b):
    k, ch = w.shape
    p = (k - 1) // 2
    y = lax.conv_general_dilated(
        x, w.astype(x.dtype)[:, None, :], window_strides=(1,), padding=[(p, p)],
        dimension_numbers=("NWC", "WIO", "NWC"), feature_group_count=ch)
    return y + b.astype(x.dtype)


def diff_attention(q, k, v, lam, subln_g, lam_init):
    bsz, s_len = q.shape[0], q.shape[1]
    nb = s_len // Q_BLOCK
    qb = q.reshape(bsz, nb, Q_BLOCK, 2 * DIFF_HEADS, DIFF_DQ).transpose(1, 0, 2, 3, 4)
    scale = DIFF_DQ ** -0.5

    def block(qblk):
        s = jnp.einsum("bqhd,bkhd->bhqk", qblk, k).astype(jnp.float32) * scale
        p = jax.nn.softmax(s, axis=-1).reshape(bsz, DIFF_HEADS, 2, Q_BLOCK, s_len)
        a = p[:, :, 0] - lam * p[:, :, 1]
        return jnp.einsum("bhqk,bkhd->bqhd", a.astype(v.dtype), v)

    o = lax.map(block, qb)
    o = o.transpose(1, 0, 2, 3, 4).reshape(bsz, s_len, DIFF_HEADS, DIFF_DV)
    o = rmsnorm(o, subln_g) * (1.0 - lam_init)
    return o.reshape(bsz, s_len, DIFF_WIDTH)


def hybrid_mixer(h, pos, w_in, conv_dw_w, conv_dw_b, conv_norm, lq1, lk1, lq2, lk2,
                 diff_subln, w_out, lam_init):
    bsz, s_len, _ = h.shape
    z = h @ w_in
    a_val, a_gate, q, k, v = jnp.split(
        z, [CONV_CH, 2 * CONV_CH, 2 * CONV_CH + DIFF_WIDTH, 2 * CONV_CH + 2 * DIFF_WIDTH], axis=-1)
    a = a_val * jax.nn.sigmoid(a_gate)
    a = dwconv(a, conv_dw_w, conv_dw_b)
    a = jax.nn.silu(rmsnorm(a, conv_norm))
    q = rope(q.reshape(bsz, s_len, 2 * DIFF_HEADS, DIFF_DQ), pos)
    k = rope(k.reshape(bsz, s_len, 2 * DIFF_HEADS, DIFF_DQ), pos)
    v = v.reshape(bsz, s_len, DIFF_HEADS, DIFF_DV)
    lam = (jnp.exp(jnp.sum(lq1.astype(jnp.float32) * lk1.astype(jnp.float32)))
           - jnp.exp(jnp.sum(lq2.astype(jnp.float32) * lk2.astype(jnp.float32))) + lam_init)
    b = diff_attention(q, k, v, lam, diff_subln, lam_init)
    return jnp.concatenate([a, b], axis=-1) @ w_out


def memory_cross_attention(h, m, w_mq, w_mkv, w_mo):
    bsz, s_len, _ = h.shape
    n = m.shape[1]
    q = (h @ w_mq).reshape(bsz, s_len, MEM_HEADS, MEM_HD)
    k, v = jnp.split(m @ w_mkv, 2, axis=-1)
    k = k.reshape(bsz, n, MEM_HEADS, MEM_HD)
    v = v.reshape(bsz, n, MEM_HEADS, MEM_HD)
    s = jnp.einsum("bqhd,bkhd->bhqk", q, k).astype(jnp.float32) * (MEM_HD ** -0.5)
    p = jax.nn.softmax(s, axis=-1).astype(v.dtype)
    o = jnp.einsum("bhqk,bkhd->bqhd", p, v).reshape(bsz, s_len, D_MODEL)
    return o @ w_mo


def conv_ffn(h, w_up, ffn_dw_w, ffn_dw_b, w_down):
    u = dwconv(h @ w_up, ffn_dw_w, ffn_dw_b)
    val, gate = jnp.split(u, 2, axis=-1)
    return (jax.nn.silu(gate) * val) @ w_down


def encoder(x, mem, norm_mix, w_in, conv_dw_w, conv_dw_b, conv_norm, lambda_q1, lambda_k1,
            lambda_q2, lambda_k2, diff_subln, w_out, norm_cross, norm_mem, w_mq, w_mkv, w_mo,
            norm_ffn, w_up, ffn_dw_w, ffn_dw_b, w_down, norm_final):
    pos = jnp.arange(x.shape[1], dtype=jnp.float32)
    for l in range(DEPTH):
        lam_init = 0.8 - 0.6 * math.exp(-0.3 * l)
        x = x + hybrid_mixer(rmsnorm(x, norm_mix[l]), pos, w_in[l], conv_dw_w[l], conv_dw_b[l],
                             conv_norm[l], lambda_q1[l], lambda_k1[l], lambda_q2[l], lambda_k2[l],
                             diff_subln[l], w_out[l], lam_init)
        x = x + memory_cross_attention(rmsnorm(x, norm_cross[l]), rmsnorm(mem, norm_mem[l]),
                                       w_mq[l], w_mkv[l], w_mo[l])
        x = x + conv_ffn(rmsnorm(x, norm_ffn[l]), w_up[l], ffn_dw_w[l], ffn_dw_b[l], w_down[l])
    return rmsnorm(x, norm_final)


def setup_inputs(seed: int = 0) -> dict:
    key = jax.random.key(seed)
    ks = jax.random.split(key, 32)
    f32 = jnp.float32

    def nrm(k, shape, scale):
        return jax.random.normal(k, shape, f32) * scale

    def gain(k, shape):
        return 1.0 + 0.05 * jax.random.normal(k, shape, f32)

    L = DEPTH
    return {
        "x_prompt": nrm(ks[0], (BATCH, SEQ, D_MODEL), 1.0),
        "x_sample": nrm(ks[1], (DEC_BATCH, DEC_SEQ, D_MODEL), 1.0),
        "mem_prompt": nrm(ks[2], (BATCH, N_MEM, D_MODEL), 1.0),
        "mem_sample": nrm(ks[3], (DEC_BATCH, N_MEM, D_MODEL), 1.0),
        "norm_mix": gain(ks[4], (L, D_MODEL)),
        "w_in": nrm(ks[5], (L, D_MODEL, IN_COLS), D_MODEL ** -0.5),
        "conv_dw_w": nrm(ks[6], (L, CONV_WIDTH, CONV_CH), CONV_WIDTH ** -0.5),
        "conv_dw_b": nrm(ks[7], (L, CONV_CH), 0.02),
        "conv_norm": gain(ks[8], (L, CONV_CH)),
        "lambda_q1": nrm(ks[9], (L, DIFF_DQ), 0.1),
        "lambda_k1": nrm(ks[10], (L, DIFF_DQ), 0.1),
        "lambda_q2": nrm(ks[11], (L, DIFF_DQ), 0.1),
        "lambda_k2": nrm(ks[12], (L, DIFF_DQ), 0.1),
        "diff_subln": gain(ks[13], (L, DIFF_DV)),
        "w_out": nrm(ks[14], (L, MIX_WIDTH, D_MODEL), MIX_WIDTH ** -0.5),
        "norm_cross": gain(ks[15], (L, D_MODEL)),
        "norm_mem": gain(ks[16], (L, D_MODEL)),
        "w_mq": nrm(ks[17], (L, D_MODEL, D_MODEL), D_MODEL ** -0.5),
        "w_mkv": nrm(ks[18], (L, D_MODEL, 2 * D_MODEL), D_MODEL ** -0.5),
        "w_mo": nrm(ks[19], (L, D_MODEL, D_MODEL), D_MODEL ** -0.5),
        "norm_ffn": gain(ks[20], (L, D_MODEL)),
        "w_up": nrm(ks[21], (L, D_MODEL, 2 * D_FF), D_MODEL ** -0.5),
        "ffn_dw_w": nrm(ks[22], (L, FFN_CONV_WIDTH, 2 * D_FF), FFN_CONV_WIDTH ** -0.5),
        "ffn_dw_b": nrm(ks[23], (L, 2 * D_FF), 0.02),
        "w_down": nrm(ks[24], (L, D_FF, D_MODEL), D_FF ** -0.5),
        "norm_final": gain(ks[25], (D_MODEL,)),
    }


def reference(x_prompt, x_sample, mem_prompt, mem_sample, norm_mix, w_in, conv_dw_w, conv_dw_b,
              conv_norm, lambda_q1, lambda_k1, lambda_q2, lambda_k2, diff_subln, w_out,
              norm_cross, norm_mem, w_mq, w_mkv, w_mo, norm_ffn, w_up, ffn_dw_w, ffn_dw_b,
              w_down, norm_final):
    y_prompt = encoder(x_prompt, mem_prompt, norm_mix, w_in, conv_dw_w, conv_dw_b, conv_norm,
                       lambda_q1, lambda_k1, lambda_q2, lambda_k2, diff_subln, w_out,
                       norm_cross, norm_mem, w_mq, w_mkv, w_mo, norm_ffn, w_up, ffn_dw_w,
                       ffn_dw_b, w_down, norm_final)
    y_sample = encoder(x_sample, mem_sample, norm_mix, w_in, conv_dw_w, conv_dw_b, conv_norm,
                       lambda_q1, lambda_k1, lambda_q2, lambda_k2, diff_subln, w_out,
                       norm_cross, norm_mem, w_mq, w_mkv, w_mo, norm_ffn, w_up, ffn_dw_w,
                       ffn_dw_b, w_down, norm_final)
    return (y_prompt, y_sample)
```

```python
import contextlib
import numpy as np
import concourse.bass as bass
import concourse.mybir as mybir
from concourse.bass_utils import run_bass_kernel_spmd

F32 = mybir.dt.float32
BF16 = mybir.dt.bfloat16
AF = mybir.ActivationFunctionType
ALU = mybir.AluOpType
AX = mybir.AxisListType


class Res:
    __slots__ = ("name", "w", "r")

    def __init__(self, name=""):
        self.name = name
        self.w = None
        self.r = {}


class DmaSem:
    __slots__ = ("sem", "count")

    def __init__(self, sem):
        self.sem = sem
        self.count = 0


class Eng:
    def __init__(self, name, sem, self_sync):
        self.name = name
        self.sem = sem
        self.self_sync = self_sync
        self.count = 0
        self.seen = {}
        self.prog = []
        self.dangling = False
        self.nwaits = 0


class Sched:
    def __init__(self, nc, stack):
        self.nc = nc
        self.stack = stack
        self.snap = {}
        self.pe = Eng("pe", self.new_sem("s_pe"), False)
        self.act = Eng("act", self.new_sem("s_act"), True)
        self.dve = Eng("dve", self.new_sem("s_dve"), True)
        self.pool = Eng("pool", self.new_sem("s_pool"), True)
        self.sp = Eng("sp", self.new_sem("s_sp"), False)
        self.engines = [self.pe, self.act, self.dve, self.pool, self.sp]

    def new_sem(self, name):
        return self.stack.enter_context(self.nc.semaphore(name))

    def dma_sem(self, name):
        return DmaSem(self.new_sem(name))

    def _deps(self, reads, writes):
        d = []
        for r in reads:
            if r.w is not None:
                d.append(r.w)
        for w in writes:
            if w.w is not None:
                d.append(w.w)
            d.extend(w.r.items())
        return d

    def _wait(self, e, deps):
        need = {}
        for sem, val in deps:
            if need.get(sem, 0) < val:
                need[sem] = val
        for sem, val in need.items():
            if sem is e.sem and not e.self_sync:
                continue
            if e.seen.get(sem, 0) >= val:
                continue
            e.prog.append(lambda h, sem=sem, val=val: h.wait_ge(sem, val))
            e.nwaits += 1
            e.seen[sem] = val
            sn = self.snap.get((sem, val))
            if sn:
                for s2, v2 in sn.items():
                    if e.seen.get(s2, 0) < v2:
                        e.seen[s2] = v2

    def _assign(self, dep, reads, writes):
        sem, val = dep
        for r in reads:
            if r.r.get(sem, 0) < val:
                r.r[sem] = val
        for w in writes:
            w.w = dep
            w.r = {}

    def op(self, e, fn, reads=(), writes=(), inc=True):
        self._wait(e, self._deps(reads, writes))
        if inc:
            e.count += 1
            val = e.count
            sem = e.sem
            e.prog.append(lambda h, fn=fn, sem=sem: fn(h).then_inc(sem, 1))
            self.snap[(sem, val)] = dict(e.seen)
            e.dangling = False
        else:
            val = e.count + 1
            e.prog.append(lambda h, fn=fn: fn(h))
            e.dangling = True
        self._assign((e.sem, val), reads, writes)

    def dma(self, q, pairs, ds, reads=(), writes=()):
        self._wait(q, self._deps(reads, writes))
        for (o, i) in pairs:
            ds.count += 16
            q.prog.append(lambda h, o=o, i=i, sem=ds.sem: h.dma_start(out=o, in_=i).then_inc(sem, 16))
        dep = (ds.sem, ds.count)
        self.snap[dep] = dict(q.seen)
        self._assign(dep, reads, writes)

    def final_wait(self, e, ress):
        d = []
        for r in ress:
            if r.w is not None:
                d.append(r.w)
            d.extend(r.r.items())
        self._wait(e, d)

    def emit(self):
        for e in self.engines:
            assert not e.dangling, e.name
        nc = self.nc
        with nc.Block() as block:
            @block.tensor
            def _(h):
                for f in self.pe.prog:
                    f(h)

            @block.scalar
            def _(h):
                for f in self.act.prog:
                    f(h)

            @block.vector
            def _(h):
                for f in self.dve.prog:
                    f(h)

            @block.gpsimd
            def _(h):
                for f in self.pool.prog:
                    f(h)

            @block.sync
            def _(h):
                for f in self.sp.prog:
                    f(h)


D = 1024
EPS = 1e-6
DFF = 2816
C_GMIX, C_GCROSS, C_GMEM, C_GFFN, C_GCONV, C_GSUB, C_CONVB, C_CONVW, C_FFNB, C_FFNW, NCOL = 0, 8, 16, 24, 32, 36, 37, 41, 165, 209, 344
import os
FORCE_OWN = int(os.environ.get("FORCE_OWN", "1000"))
NWCH = 23


class Arena:
    def __init__(self, nc, st, kib):
        self.t = st.enter_context(nc.sbuf_tensor("arena", [128, kib * 512], BF16))
        self.limit = kib * 1024
        self.top = 0

    def alloc(self, shape, dt):
        n = 1
        for s in shape:
            n *= s
        esz = 4 if dt == F32 else 2
        nb = (n * esz + 31) // 32 * 32
        off = self.top
        assert off + nb <= self.limit, ("SBUF arena overflow", off + nb, self.limit)
        self.top += nb
        self.hw = max(getattr(self, 'hw', 0), self.top)
        a = self.t[:, off // 2: off // 2 + n * esz // 2]
        if dt == F32:
            a = a.bitcast(F32)
        if len(shape) == 2:
            a = a.rearrange("p (a b) -> p a b", a=shape[0])
        elif len(shape) == 3:
            a = a.rearrange("p (a b c) -> p a b c", a=shape[0], b=shape[1])
        return a


class _Stop(Exception):
    pass


def build(TP, TS, stop=None):
    TSO = TS // 2
    stg_i = [1]

    def checkpoint(name):
        stg_i[0] += 1
        if stop is not None and stg_i[0] >= stop:
            print('STOP at', stg_i[0], name)
            raise _Stop()
    nc = bass.Bass("TRN2", target_bir_lowering=False)

    def din(name, shape, dt=F32):
        return nc.dram_tensor(name, shape, dt, kind="ExternalInput").ap()

    xp = din("xp", [TP, D]); xs = din("xs", [TS, D])
    memp = din("memp", [256, D]); mems = din("mems", [256, D])
    cstp = din("cstp", [TP // 128, 128, 96]); csts = din("csts", [TS // 128, 128, 96])
    identd = din("ident", [128, 128])
    cols = din("cols", [128, NCOL])
    lamv = din("lamv", [4, 64])
    gfin = din("gfin", [D])
    w_in = din("w_in", [D, 2560]); w_out = din("w_out", [D, D]); w_mq = din("w_mq", [D, D])
    w_mkv = din("w_mkv", [D, 2 * D]); w_mo = din("w_mo", [D, D])
    w_up = din("w_up", [D, 2 * DFF]); w_down = din("w_down", [DFF, D])
    yp = nc.dram_tensor("yp", [TP, D], F32, kind="ExternalOutput").ap()
    ys = nc.dram_tensor("ys", [TSO, D], F32, kind="ExternalOutput").ap()
    wscr = nc.dram_tensor("wscr", [NWCH, 128, 4096], BF16, kind="Internal").ap()

    with contextlib.ExitStack() as st:
        S = Sched(nc, st)
        S.dsems = []

        def dsem(name):
            d = S.dma_sem(name)
            S.dsems.append(d)
            return d

        def barrier():
            for e in S.engines:
                assert not e.dangling
            for e in S.engines:
                deps = [(o.sem, o.count) for o in S.engines if o is not e and o.count > 0]
                deps += [(d.sem, d.count) for d in S.dsems if d.count > 0]
                S._wait(e, deps)

        ar = Arena(nc, st, 206)
        ps = st.enter_context(nc.psum_tensor("ps", [128, 8, 512], F32))
        rps = [Res("ps%d" % i) for i in range(8)]

        def psb(i):
            return ps[:, i, :].bitcast(BF16)

        def ACT(out, in_, func, reads, writes, **kw):
            S.op(S.act, lambda h: h.activation(out=out, in_=in_, func=func, **kw), reads, writes)

        def MM(out, lhsT, rhs, start, stop, reads, writes, inc=True):
            S.op(S.pe, lambda h: h.matmul(out, lhsT=lhsT, rhs=rhs, start=start, stop=stop), reads, writes, inc)

        def TR(out, in_, reads, writes, inc=True):
            S.op(S.pe, lambda h: h.transpose(out=out, in_=in_, identity=identb), reads, writes, inc)

        def TSC(eng, out, in0, s1, s2, op0, op1, reads, writes):
            if s2 is None:
                S.op(eng, lambda h: h.tensor_scalar(out=out, in0=in0, scalar1=s1, scalar2=None, op0=op0), reads, writes)
            else:
                S.op(eng, lambda h: h.tensor_scalar(out=out, in0=in0, scalar1=s1, scalar2=s2, op0=op0, op1=op1), reads, writes)

        def TT(eng, out, in0, in1, op, reads, writes):
            S.op(eng, lambda h: h.tensor_tensor(out=out, in0=in0, in1=in1, op=op), reads, writes)

        def STT(out, in0, scalar, in1, op0, op1, reads, writes):
            S.op(S.dve, lambda h: h.scalar_tensor_tensor(out=out, in0=in0, scalar=scalar, in1=in1, op0=op0, op1=op1), reads, writes)

        def CP(eng, out, in_, reads, writes):
            if eng is S.act:
                ACT(out, in_, AF.Copy, reads, writes)
            else:
                S.op(eng, lambda h: h.tensor_copy(out=out, in_=in_), reads, writes)

        def MSET(eng, ap, val, writes):
            S.op(eng, lambda h: h.memset(ap, val), (), writes)

        def RECIP(out, in_, reads, writes):
            S.op(S.dve, lambda h: h.reciprocal(out=out, in_=in_), reads, writes)

        identf = ar.alloc([128], F32); identb = ar.alloc([128], BF16); onesb = ar.alloc([128], BF16)
        onesf = ar.alloc([128], F32)
        colt = ar.alloc([NCOL], F32)
        epsb = ar.alloc([1], F32); nlam = ar.alloc([1], F32)
        gfb = ar.alloc([D], F32)
        r_const = Res("const")
        d_const = dsem("d_const")
        S.dma(S.sp, [(identf, identd), (colt, cols), (gfb, gfin.partition_broadcast(128))], d_const, writes=[r_const])
        CP(S.dve, identb, identf, [r_const], [r_const])
        MSET(S.pool, onesb, 1.0, [r_const])
        MSET(S.pool, onesf, 1.0, [r_const])
        MSET(S.pool, epsb, EPS, [r_const])
        LAM_INIT = 0.2
        mk0 = ar.top
        lv = ar.alloc([4, 64], F32); lm = ar.alloc([2, 64], F32); lsum = ar.alloc([2], F32); lexp = ar.alloc([2], F32)
        r_l = Res("lam")
        S.dma(S.sp, [(lv[:, i, :], lamv[i].partition_broadcast(128)) for i in range(4)], d_const, writes=[r_l])
        TT(S.dve, lm[:, 0, :], lv[:, 0, :], lv[:, 1, :], ALU.mult, [r_l], [r_l])
        TT(S.dve, lm[:, 1, :], lv[:, 2, :], lv[:, 3, :], ALU.mult, [r_l], [r_l])
        for i in range(2):
            ACT(lv[:, i, :], lm[:, i, :], AF.Copy, [r_l], [r_l], accum_out=lsum[:, i:i + 1])
        ACT(lexp, lsum, AF.Exp, [r_l], [r_l])
        TT(S.dve, lexp[:, 0:1], lexp[:, 0:1], lexp[:, 1:2], ALU.subtract, [r_l], [r_l])
        TSC(S.dve, nlam, lexp[:, 0:1], -1.0, -LAM_INIT, ALU.mult, ALU.add, [r_l], [r_const])
        TSC(S.dve, colt[:, C_GSUB:C_GSUB + 1], colt[:, C_GSUB:C_GSUB + 1], 1.0 - LAM_INIT, None, ALU.mult, None, [r_const], [r_const])
        TSC(S.dve, colt[:, C_CONVW:C_CONVW + 124], colt[:, C_CONVW:C_CONVW + 124], 0.5, None, ALU.mult, None, [r_const], [r_const])
        barrier()
        ar.top = mk0
        base_top = ar.top
        setup_stop = (stop == 1)

        def col(c0, n=1):
            return colt[:, c0:c0 + n]

        r_out = Res("out")
        d_out = [dsem("d_out%d" % i) for i in range(2)]
        try:
            if setup_stop:
                raise _Stop()
            cv_cnt = [0]

            def convert(out, in_, scale_ap, const, reads, writes):
                cv_cnt[0] += 1
                if cv_cnt[0] % 2 == 0:
                    if scale_ap is None:
                        TSC(S.dve, out, in_, const, None, ALU.mult, None, reads, writes)
                    elif const == 1.0:
                        TSC(S.dve, out, in_, scale_ap, None, ALU.mult, None, reads, writes)
                    else:
                        raise AssertionError('mixed scale')
                else:
                    if scale_ap is None:
                        ACT(out, in_, AF.Copy, reads, writes, scale=const)
                    elif const == 1.0:
                        ACT(out, in_, AF.Identity, reads, writes, scale=scale_ap)
                    else:
                        raise AssertionError('mixed scale')

            r_wscr = Res("wscr")
            stg = [ar.alloc([4096], F32) for _ in range(2)]
            cvb = [ar.alloc([4096], BF16) for _ in range(2)]
            r_stg = [Res("stg%d" % i) for i in range(2)]
            r_cvb = [Res("cvb%d" % i) for i in range(2)]
            d_stg = [dsem("d_stg%d" % i) for i in range(2)]
            d_cvb = [dsem("d_cvb%d" % i) for i in range(2)]

            def kview(w, r0, nk, c0, nc_):
                return w[r0:r0 + nk * 128, c0:c0 + nc_].rearrange("(k p) n -> p k n", p=128)

            for ci in range(NWCH):
                k = ci % 2
                sv, cvv = stg[k], cvb[k]
                if ci < 6:
                    wsrc = [w_out, w_mq, w_mo][ci // 2]
                    ch = ci % 2
                    s3 = sv.rearrange("p (k n) -> p k n", k=8)
                    c3 = cvv.rearrange("p (k n) -> p k n", k=8)
                    S.dma(S.sp, [(s3, kview(wsrc, 0, 8, ch * 512, 512))], d_stg[k], writes=[r_stg[k]])
                    for kc in range(8):
                        if ci < 2:
                            sc, cst = (None, 0.5) if kc < 4 else (col(C_GSUB), 1.0)
                        elif ci < 4:
                            sc, cst = col(C_GCROSS + kc), 1.0
                        else:
                            sc, cst = None, 1.0
                        convert(c3[:, kc, :], s3[:, kc, :], sc, cst, [r_stg[k], r_const], [r_cvb[k]])
                elif ci < 17:
                    cu = ci - 6
                    s4 = sv.rearrange("p (a k n) -> p a k n", a=2, k=8)
                    c4 = cvv.rearrange("p (a k n) -> p a k n", a=2, k=8)
                    pairs = []
                    for pp in range(2):
                        cf = 2 * cu + pp
                        pairs.append((s4[:, pp, :, 0:128], kview(w_up, 0, 8, cf * 128, 128)))
                        pairs.append((s4[:, pp, :, 128:256], kview(w_up, 0, 8, DFF + cf * 128, 128)))
                    S.dma(S.sp, pairs, d_stg[k], writes=[r_stg[k]])
                    for kc in range(8):
                        convert(c4[:, :, kc, :], s4[:, :, kc, :], col(C_GFFN + kc), 1.0, [r_stg[k], r_const], [r_cvb[k]])
                else:
                    j = ci - 17
                    ch, kci = j // 3, j % 3
                    nk = 8 if kci < 2 else 6
                    s3 = sv.rearrange("p (k n) -> p k n", k=8)
                    c3 = cvv.rearrange("p (k n) -> p k n", k=8)
                    S.dma(S.sp, [(s3[:, 0:nk, :], kview(w_down, kci * 1024, nk, ch * 512, 512))], d_stg[k], writes=[r_stg[k]])
                    convert(c3[:, 0:nk, :], s3[:, 0:nk, :], None, 0.5, [r_stg[k]], [r_cvb[k]])
                S.dma(S.pool, [(wscr[ci], cvv)], d_cvb[k], reads=[r_cvb[k]], writes=[r_wscr])
            barrier()
            checkpoint('prepass')
            ar.top = base_top

            junk = ar.alloc([D], BF16)
            n_ss = [ar.alloc([1], F32) for _ in range(4)]
            n_ln = [ar.alloc([1], F32) for _ in range(4)]
            n_rs = [ar.alloc([1], F32) for _ in range(4)]
            n_xn = [ar.alloc([D], BF16) for _ in range(3)]
            r_ss = [Res() for _ in range(4)]; r_ln = [Res() for _ in range(4)]
            r_rs = [Res() for _ in range(4)]; r_xn = [Res() for _ in range(3)]
            ncnt = [0]

            def rms_stats(x_ap, rx, k):
                ACT(junk, x_ap, AF.Square, [rx], [r_ss[k]], accum_out=n_ss[k])
                ACT(n_ln[k], n_ss[k], AF.Ln, [r_ss[k], r_const], [r_ln[k]], scale=1.0 / D, bias=epsb)
                ACT(n_rs[k], n_ln[k], AF.Exp, [r_ln[k]], [r_rs[k]], scale=-0.5)

            def rms_scale(x_ap, rx, k, kx):
                TSC(S.dve, n_xn[kx], x_ap, n_rs[k], None, ALU.mult, None, [rx, r_rs[k]], [r_xn[kx]])

            def rms_tr(kx, trbank):
                pb = psb(trbank)
                for c in range(8):
                    TR(pb[:, c * 128:(c + 1) * 128], n_xn[kx][:, c * 128:(c + 1) * 128], [r_xn[kx], r_const], [rps[trbank]], inc=(c == 7))

            def rms_evac(evac, trbank, dstT, rdst, col0):
                CP(evac, dstT[:, :, col0:col0 + 128], psb(trbank).rearrange("p (a b) -> p a b", a=8), [rps[trbank]], [rdst])

            def rms_T(x_ap, rx, dstT, rdst, col0, trbank, evac):
                k, kx = ncnt[0] % 4, ncnt[0] % 3
                ncnt[0] += 1
                rms_stats(x_ap, rx, k)
                rms_scale(x_ap, rx, k, kx)
                rms_tr(kx, trbank)
                rms_evac(evac, trbank, dstT, rdst, col0)

            def run_pipeline(stages, n):
                lo = -max(o for _, o in stages)
                hi = n - min(o for _, o in stages)
                for i in range(lo, hi):
                    for fn, off in stages:
                        if 0 <= i + off < n:
                            fn(i + off)

            norm_top = ar.top

            groups = [
                dict(name="p", x=xp, y=yp, TK=TP, TOWN=TP, NQ=TP // 128, halo=False, cst=cstp, mem=memp),
                dict(name="s", x=xs, y=ys, TK=TS, TOWN=TSO, NQ=TSO // 128 + 1, halo=True, cst=csts, mem=mems),
            ]
            d_x = [dsem("d_x%d" % i) for i in range(5)]
            d_xt = [dsem("d_xt%d" % i) for i in range(4)]
            d_cs = [dsem("d_cs%d" % i) for i in range(5)]
            d_w = [dsem("d_w%d" % i) for i in range(3)]
            d_sg = dsem("d_sg")

            for g in groups:
                ar.top = norm_top
                gx, TK, TOWN, NQ = g["x"], g["TK"], g["TOWN"], g["NQ"]
                NK = TK // 128
                TQ = NQ * 128
                BT = ar.alloc([4, TQ + 2], BF16)
                r_AT = Res("AT"); r_BT = Res("BT")
                KmT = ar.alloc([8, 256], BF16); Vm = ar.alloc([2, D], BF16)
                r_Km = Res("KmT"); r_Vm = Res("Vm")
                MSET(S.pool, BT[:, :, 0:1], 0.0, [r_BT]); MSET(S.pool, BT[:, :, TQ + 1:TQ + 2], 0.0, [r_BT])
                grp_top = ar.top

                xr = [ar.alloc([D], F32) for _ in range(3)]
                r_xr = [Res() for _ in range(3)]
                memT = ar.alloc([8, 256], BF16); r_memT = Res()
                sg = ar.alloc([8, 256], F32); r_sg = Res()
                wp = ar.alloc([8, 256], BF16); r_wp = Res()
                for t in range(2):
                    S.dma(S.sp, [(xr[t], g["mem"][t * 128:(t + 1) * 128, :])], d_x[t], writes=[r_xr[t]])
                    rms_T(xr[t], r_xr[t], memT, r_memT, t * 128, t, S.act)
                for j in range(8):
                    S.dma(S.sp, [(sg, kview(w_mkv, 0, 8, j * 256, 256))], d_sg, writes=[r_sg])
                    for kc in range(8):
                        convert(wp[:, kc, :], sg[:, kc, :], col(C_GMEM + kc), 1.0, [r_sg, r_const], [r_wp])
                    if j < 4:
                        for f2 in range(2):
                            fc = 2 * j + f2
                            bk = 2 + (fc % 2)
                            for kc in range(8):
                                MM(ps[:, bk, 0:256], wp[:, kc, f2 * 128:(f2 + 1) * 128], memT[:, kc, :], kc == 0, kc == 7,
                                   [r_wp, r_memT], [rps[bk]], inc=(kc == 7))
                            CP(S.act, KmT[:, fc, :], ps[:, bk, 0:256], [rps[bk]], [r_Km])
                    else:
                        for kc2 in range(2):
                            bk = 2 + kc2
                            for kc in range(8):
                                MM(ps[:, bk, 0:256], memT[:, kc, kc2 * 128:(kc2 + 1) * 128], wp[:, kc, :], kc == 0, kc == 7,
                                   [r_wp, r_memT], [rps[bk]], inc=(kc == 7))
                            CP(S.dve, Vm[:, kc2, (j - 4) * 256:(j - 3) * 256], ps[:, bk, 0:256], [rps[bk]], [r_Vm])
                barrier()
                checkpoint('stageM')
                ar.top = grp_top

                for hp in range(2):
                    ar.top = grp_top
                    KT = ar.alloc([2, TK], BF16); VV = ar.alloc([NK, 256], BF16); QT = ar.alloc([2, TQ], BF16)
                    r_KT = Res("KT"); r_VV = Res("V"); r_QT = Res("QT")
                    passA_top = ar.top
                    Wq = ar.alloc([8, 768], BF16); r_Wq = Res("Wq")
                    sg = ar.alloc([8, 256], F32); r_sg = Res()
                    for pc in range(3):
                        c0 = 1024 + 512 * pc + 256 * hp
                        S.dma(S.sp, [(sg, kview(w_in, 0, 8, c0, 256))], d_sg, writes=[r_sg])
                        for kc in range(8):
                            convert(Wq[:, kc, pc * 256:(pc + 1) * 256], sg[:, kc, :], col(C_GMIX + kc), 1.0, [r_sg, r_const], [r_Wq])
                    xr = [ar.alloc([D], F32) for _ in range(5)]
                    r_xr = [Res() for _ in range(5)]
                    csr = [ar.alloc([3, 32], F32) for _ in range(5)]
                    r_cs = [Res() for _ in range(5)]
                    xnT = [ar.alloc([8, 128], BF16) for _ in range(2)]
                    r_xnT = [Res() for _ in range(2)]
                    tA = [ar.alloc([512], F32) for _ in range(2)]; tB = [ar.alloc([512], F32) for _ in range(2)]
                    rk = [ar.alloc([512], BF16) for _ in range(2)]
                    r_tA = [Res() for _ in range(2)]; r_tB = [Res() for _ in range(2)]; r_rk = [Res() for _ in range(2)]
                    def s_load(t):
                        k5 = t % 5
                        S.dma(S.sp, [(xr[k5], gx[t * 128:(t + 1) * 128, :])], d_x[k5], writes=[r_xr[k5]])
                        S.dma(S.sp, [(csr[k5], g["cst"][t].rearrange("p (a f) -> p a f", a=3))], d_cs[k5], writes=[r_cs[k5]])

                    def s_stats(t):
                        rms_stats(xr[t % 5], r_xr[t % 5], t % 4)

                    def s_scale(t):
                        rms_scale(xr[t % 5], r_xr[t % 5], t % 4, t % 3)

                    def s_tr(t):
                        rms_tr(t % 3, t % 2)

                    def s_evac(t):
                        rms_evac(S.act, t % 2, xnT[t % 2], r_xnT[t % 2], 0)

                    def s_mm(t):
                        k2 = t % 2
                        bA, bB = 2 + k2, 4 + k2
                        for kc in range(8):
                            MM(ps[:, bA, :], xnT[k2][:, kc, :], Wq[:, kc, 0:512], kc == 0, kc == 7, [r_xnT[k2], r_Wq], [rps[bA]], inc=False)
                        for kc in range(8):
                            MM(ps[:, bB, 0:256], xnT[k2][:, kc, :], Wq[:, kc, 512:768], kc == 0, kc == 7, [r_xnT[k2], r_Wq], [rps[bB]], inc=(kc == 7))

                    def s_vcopy(t):
                        CP(S.act, VV[:, t, :], ps[:, 4 + t % 2, 0:256], [rps[4 + t % 2]], [r_VV])

                    def s_rope(t):
                        k2, k5 = t % 2, t % 5
                        bA = 2 + k2
                        xv = ps[:, bA, :].rearrange("p (a b f) -> p a b f", a=8, b=2)
                        av = tA[k2].rearrange("p (a b f) -> p a b f", a=8, b=2)
                        bv = tB[k2].rearrange("p (a b f) -> p a b f", a=8, b=2)
                        cosb = csr[k5][:, 0, :].unsqueeze(1).unsqueeze(1).to_broadcast([128, 8, 2, 32])
                        sinb = csr[k5][:, 1, :].unsqueeze(1).to_broadcast([128, 8, 32])
                        nsinb = csr[k5][:, 2, :].unsqueeze(1).to_broadcast([128, 8, 32])
                        TT(S.dve, av, xv, cosb, ALU.mult, [rps[bA], r_cs[k5]], [r_tA[k2]])
                        TT(S.dve, bv[:, :, 0, :], xv[:, :, 1, :], nsinb, ALU.mult, [rps[bA], r_cs[k5]], [r_tB[k2]])
                        TT(S.dve, bv[:, :, 1, :], xv[:, :, 0, :], sinb, ALU.mult, [rps[bA], r_cs[k5]], [r_tB[k2]])

                    def s_add(t):
                        k2 = t % 2
                        TT(S.pool, rk[k2], tA[k2], tB[k2], ALU.add, [r_tA[k2], r_tB[k2]], [r_rk[k2]])

                    def s_rktr(t):
                        k2 = t % 2
                        pbt = psb(6 + k2)
                        for c in range(4):
                            TR(pbt[:, c * 128:(c + 1) * 128], rk[k2][:, c * 128:(c + 1) * 128], [r_rk[k2], r_const], [rps[6 + k2]], inc=(c == 3))

                    def s_copies(t):
                        k2 = t % 2
                        p3 = psb(6 + k2)[:, 0:512].rearrange("p (a b) -> p a b", a=4)
                        if t < NQ:
                            CP(S.dve, QT[:, :, t * 128:(t + 1) * 128], p3[:, 0:2, :], [rps[6 + k2]], [r_QT])
                        CP(S.dve, KT[:, :, t * 128:(t + 1) * 128], p3[:, 2:4, :], [rps[6 + k2]], [r_KT])

                    run_pipeline([(s_load, 3), (s_stats, 3), (s_scale, 2), (s_tr, 1), (s_evac, 1), (s_mm, 0), (s_vcopy, -1),
                                  (s_rope, -1), (s_add, -1), (s_rktr, -2), (s_copies, -3)], NK)
                    barrier()
                    checkpoint('stageA')
                    ar.top = passA_top
                    NEB = 4
                    Eb = [ar.alloc([2, 512], BF16) for _ in range(NEB)]
                    r_E = [Res() for _ in range(NEB)]
                    Esd = ar.alloc([2, 512], F32)
                    r_Esd = Res()
                    rD = [ar.alloc([512], F32) for _ in range(2)]; r_rD = [Res() for _ in range(2)]
                    t0 = ar.alloc([512], F32); r_t0 = Res()
                    Rr = ar.alloc([512], F32); r_R = Res()
                    Rsq = ar.alloc([512], BF16); r_Rsq = Res()
                    lnv = ar.alloc([512], F32); r_lnv = Res()
                    rsd = ar.alloc([512], F32); r_rsd = Res()
                    qgs = [(q0, min(512, TQ - q0)) for q0 in range(0, TQ, 512)]
                    ecnt = 0
                    pending = [None]

                    def make_deferred(h_, q0_, nq_):
                        def d():
                            MM(ps[:, 6, 0:nq_], onesb, Rsq[:, 0:nq_], True, True, [r_const, r_Rsq], [rps[6]])
                            ACT(lnv[:, 0:nq_], ps[:, 6, 0:nq_], AF.Ln, [rps[6], r_const], [r_lnv], scale=1.0 / 128, bias=epsb)
                            ACT(rsd[:, 0:nq_], lnv[:, 0:nq_], AF.Exp, [r_lnv], [r_rsd], scale=-0.5)
                            TT(S.dve, BT[:, h_, 1 + q0_:1 + q0_ + nq_], Rr[:, 0:nq_], rsd[:, 0:nq_], ALU.mult, [r_R, r_rsd], [r_BT])
                        return d
                    for hl in range(2):
                        h = 2 * hp + hl
                        for (q0, nq) in qgs:
                            def qk(kt, sb_):
                                b0 = 2 * sb_
                                MM(ps[:, b0, 0:nq], KT[0:64, hl, kt * 128:(kt + 1) * 128], QT[0:64, hl, q0:q0 + nq], True, True,
                                   [r_KT, r_QT], [rps[b0]], inc=False)
                                MM(ps[:, b0 + 1, 0:nq], KT[64:128, hl, kt * 128:(kt + 1) * 128], QT[64:128, hl, q0:q0 + nq], True, True,
                                   [r_KT, r_QT], [rps[b0 + 1]], inc=True)
                            qk(0, 0)
                            for kt in range(NK):
                                sb_ = kt % 2
                                b0 = 2 * sb_
                                e = ecnt % NEB
                                ecnt += 1
                                ACT(Eb[e][:, :, 0:nq], ps[:, b0:b0 + 2, 0:nq], AF.Exp, [rps[b0], rps[b0 + 1]], [r_E[e]], scale=0.125)
                                if kt + 1 < NK:
                                    qk(kt + 1, 1 - sb_)
                                st_, sp_ = (kt == 0), (kt == NK - 1)
                                vv = VV[:, kt, hl * 128:(hl + 1) * 128]
                                MM(ps[:, 4, 0:nq], vv, Eb[e][:, 0, 0:nq], st_, sp_, [r_VV, r_E[e]], [rps[4]], inc=False)
                                MM(ps[:, 5, 0:nq], vv, Eb[e][:, 1, 0:nq], st_, sp_, [r_VV, r_E[e]], [rps[5]], inc=True)
                                if kt == 5 and pending[0] is not None:
                                    pending[0]()
                                    pending[0] = None
                                if kt % 8 == 7:
                                    MM(ps[:, 6, 0:nq], onesb, Eb[e][:, 0, 0:nq], kt == 7, False, [r_const, r_E[e]], [rps[6]], inc=False)
                                    MM(ps[:, 7, 0:nq], onesb, Eb[e][:, 1, 0:nq], kt == 7, False, [r_const, r_E[e]], [rps[7]], inc=True)
                                elif kt == 0:
                                    CP(S.dve, Esd[:, :, 0:nq], Eb[e][:, :, 0:nq], [r_E[e]], [r_Esd])
                                else:
                                    TT(S.dve, Esd[:, :, 0:nq], Esd[:, :, 0:nq], Eb[e][:, :, 0:nq], ALU.add, [r_E[e], r_Esd], [r_Esd])
                            MM(ps[:, 6, 0:nq], onesf, Esd[:, 0, 0:nq], NK < 8, True, [r_const, r_Esd], [rps[6]], inc=False)
                            MM(ps[:, 7, 0:nq], onesf, Esd[:, 1, 0:nq], NK < 8, True, [r_const, r_Esd], [rps[7]], inc=True)
                            ACT(rD[0][:, 0:nq], ps[:, 6, 0:nq], AF.Ln, [rps[6]], [r_rD[0]])
                            ACT(rD[1][:, 0:nq], ps[:, 7, 0:nq], AF.Ln, [rps[7]], [r_rD[1]])
                            CP(S.dve, t0[:, 0:nq], ps[:, 4, 0:nq], [rps[4]], [r_t0])
                            CP(S.dve, Rr[:, 0:nq], ps[:, 5, 0:nq], [rps[5]], [r_R])
                            ACT(rD[0][:, 0:nq], rD[0][:, 0:nq], AF.Exp, [r_rD[0]], [r_rD[0]], scale=-1.0)
                            ACT(rD[1][:, 0:nq], rD[1][:, 0:nq], AF.Exp, [r_rD[1]], [r_rD[1]], scale=-1.0)
                            TT(S.dve, t0[:, 0:nq], t0[:, 0:nq], rD[0][:, 0:nq], ALU.mult, [r_t0, r_rD[0]], [r_t0])
                            TT(S.dve, Rr[:, 0:nq], Rr[:, 0:nq], rD[1][:, 0:nq], ALU.mult, [r_R, r_rD[1]], [r_R])
                            STT(Rr[:, 0:nq], Rr[:, 0:nq], nlam, t0[:, 0:nq], ALU.mult, ALU.add, [r_R, r_t0, r_const], [r_R])
                            TT(S.dve, Rsq[:, 0:nq], Rr[:, 0:nq], Rr[:, 0:nq], ALU.mult, [r_R], [r_Rsq])
                            pending[0] = make_deferred(h, q0, nq)
                    if pending[0] is not None:
                        pending[0]()
                        pending[0] = None
                    barrier()
                    checkpoint('stageB')

                ar.top = grp_top
                AT = ar.alloc([4, TQ + 2], BF16)
                MSET(S.pool, AT[:, :, 0:1], 0.0, [r_AT]); MSET(S.pool, AT[:, :, TQ + 1:TQ + 2], 0.0, [r_AT])
                at_top = ar.top
                Dg = ar.alloc([124, 128], BF16); r_Dg = Res("Dg")
                for c in range(4):
                    for k in range(31):
                        TSC(S.dve, Dg[:, c * 31 + k, :], identf, col(C_CONVW + c * 31 + k), None, ALU.mult, None, [r_const], [r_Dg])
                GW = TQ + 30
                G = ar.alloc([4, GW], BF16); r_G = Res("G")
                MSET(S.pool, G[:, :, 0:15], 0.0, [r_G]); MSET(S.pool, G[:, :, 15 + TQ:GW], 0.0, [r_G])
                c0_top = ar.top
                Wa = ar.alloc([8, 1024], BF16); r_Wa = Res("Wa")
                sg = ar.alloc([8, 256], F32); r_sg = Res()
                for pc in range(4):
                    S.dma(S.sp, [(sg, kview(w_in, 0, 8, pc * 256, 256))], d_sg, writes=[r_sg])
                    for kc in range(8):
                        convert(Wa[:, kc, pc * 256:(pc + 1) * 256], sg[:, kc, :], col(C_GMIX + kc), 1.0, [r_sg, r_const], [r_Wa])
                xr = [ar.alloc([D], F32) for _ in range(5)]
                r_xr = [Res() for _ in range(5)]
                xnT = [ar.alloc([8, 128], BF16) for _ in range(2)]
                r_xnT = [Res() for _ in range(2)]
                th = [ar.alloc([512], F32) for _ in range(2)]; r_th = [Res() for _ in range(2)]
                def c_load(t):
                    S.dma(S.sp, [(xr[t % 5], gx[t * 128:(t + 1) * 128, :])], d_x[t % 5], writes=[r_xr[t % 5]])

                def c_stats(t):
                    rms_stats(xr[t % 5], r_xr[t % 5], t % 4)

                def c_scale(t):
                    rms_scale(xr[t % 5], r_xr[t % 5], t % 4, t % 3)

                def c_tr(t):
                    rms_tr(t % 3, t % 2)

                def c_evac(t):
                    rms_evac(S.act, t % 2, xnT[t % 2], r_xnT[t % 2], 0)

                def c_mm(t):
                    k2 = t % 2
                    bV, bG = 2 + k2, 4 + k2
                    for fc in range(8):
                        bk = bV if fc < 4 else bG
                        for kc in range(8):
                            MM(ps[:, bk, (fc % 4) * 128:(fc % 4 + 1) * 128], Wa[:, kc, fc * 128:(fc + 1) * 128], xnT[k2][:, kc, :],
                               kc == 0, kc == 7, [r_Wa, r_xnT[k2]], [rps[bk]], inc=(kc == 7 and fc % 4 == 3))

                def c_glu(t):
                    k2 = t % 2
                    bV, bG = 2 + k2, 4 + k2
                    ACT(th[k2], ps[:, bG, :], AF.Tanh, [rps[bG]], [r_th[k2]], scale=0.5)
                    STT(G[:, :, 15 + t * 128:15 + (t + 1) * 128], th[k2].rearrange("p (a b) -> p a b", a=4), 1.0,
                        ps[:, bV, :].rearrange("p (a b) -> p a b", a=4), ALU.add, ALU.mult, [r_th[k2], rps[bV]], [r_G])

                run_pipeline([(c_load, 3), (c_stats, 3), (c_scale, 2), (c_tr, 1), (c_evac, 1), (c_mm, 0), (c_glu, -1)], NQ)
                barrier()
                ar.top = c0_top
                cvb_ = ar.alloc([4, 512], F32); r_cv = Res()
                sq = ar.alloc([4, 512], BF16); r_sq = Res()
                lnv = ar.alloc([512], F32); r_lnv = Res()
                rsd = ar.alloc([512], F32); r_rsd = Res()
                th2 = ar.alloc([4, 512], F32); r_th2 = Res()
                for c0 in range(0, TQ, 512):
                    n = min(512, TQ - c0)
                    for c in range(4):
                        for k in range(31):
                            MM(ps[:, 4 + c, 0:n], Dg[:, c * 31 + k, :], G[:, c, c0 + k:c0 + k + n], k == 0, k == 30,
                               [r_Dg, r_G], [rps[4 + c]], inc=(k == 30))
                    for c in range(4):
                        ACT(cvb_[:, c, 0:n], ps[:, 4 + c, 0:n], AF.Identity, [rps[4 + c], r_const], [r_cv], bias=col(C_CONVB + c))
                        ACT(sq[:, c, 0:n], ps[:, 4 + c, 0:n], AF.Square, [rps[4 + c], r_const], [r_sq], bias=col(C_CONVB + c))
                    for c in range(4):
                        MM(ps[:, 2, 0:n], onesb, sq[:, c, 0:n], c == 0, c == 3, [r_const, r_sq], [rps[2]], inc=(c == 3))
                    ACT(lnv[:, 0:n], ps[:, 2, 0:n], AF.Ln, [rps[2], r_const], [r_lnv], scale=1.0 / 512, bias=epsb)
                    ACT(rsd[:, 0:n], lnv[:, 0:n], AF.Exp, [r_lnv], [r_rsd], scale=-0.5)
                    for c in range(4):
                        STT(cvb_[:, c, 0:n], cvb_[:, c, 0:n], col(C_GCONV + c), rsd[:, 0:n], ALU.mult, ALU.mult, [r_cv, r_rsd, r_const], [r_cv])
                    ACT(th2[:, :, 0:n], cvb_[:, :, 0:n], AF.Tanh, [r_cv], [r_th2], scale=0.5)
                    STT(AT[:, :, 1 + c0:1 + c0 + n], th2[:, :, 0:n], 1.0, cvb_[:, :, 0:n], ALU.add, ALU.mult, [r_th2, r_cv], [r_AT])
                barrier()
                checkpoint('C0')

                ar.top = at_top
                NB = TOWN // 510
                blocks = [(510 * b - 1, 4, 510 * b) for b in range(NB)]
                blocks.append((TOWN - 127, 1, 510 * NB))
                xt = ar.alloc([4, D], F32); r_xt = [Res() for _ in range(4)]
                xin = ar.alloc([4, D], F32); r_xin = [Res() for _ in range(4)]
                WR = [ar.alloc([4096], BF16) for _ in range(3)]; r_WR = [Res() for _ in range(3)]
                hT = ar.alloc([8, 512], BF16); r_hT = Res("hT")
                qm2 = [ar.alloc([2, 512], BF16) for _ in range(2)]; r_qm2 = [Res(), Res()]
                omT = ar.alloc([8, 512], BF16); r_om = [Res() for _ in range(8)]
                Em = [ar.alloc([512], BF16) for _ in range(2)]; r_Em = [Res() for _ in range(2)]
                rDm = ar.alloc([512], F32); r_rDm = Res()
                gT = ar.alloc([22, 512], BF16); r_gT = [Res("gT%d" % i) for i in range(22)]
                accv2 = [ar.alloc([512], F32) for _ in range(2)]; accg2 = [ar.alloc([512], F32) for _ in range(2)]
                tg2 = [ar.alloc([512], F32) for _ in range(2)]
                r_av2 = [Res() for _ in range(2)]; r_ag2 = [Res() for _ in range(2)]; r_tg2 = [Res() for _ in range(2)]
                f_ss = ar.alloc([1], F32); f_ln = ar.alloc([1], F32); f_rs = ar.alloc([1], F32)
                r_fs = Res()
                MSET(S.pool, gT, 0.0, r_gT)
                wcnt = [0]

                def wload(ci):
                    k = wcnt[0] % 3
                    wcnt[0] += 1
                    S.dma(S.sp, [(WR[k], wscr[ci])], d_w[k], reads=[r_wscr], writes=[r_WR[k]])
                    return k

                pbk = [1, 2, 3, 4]
                ycl = [0]
                for bi, (w0, nt, tok0) in enumerate(blocks):
                    N = nt * 128
                    last = bi == len(blocks) - 1
                    c1 = 1 + w0
                    def load_x(w0_, nt_):
                        for j in range(nt_):
                            ta, tb = w0_ + 128 * j, w0_ + 128 * (j + 1)
                            lo, hi = max(ta, 0), min(tb, TK)
                            if lo > ta or hi < tb:
                                MSET(S.pool, xin[:, j, :], 0.0, [r_xin[j]])
                            S.dma(S.sp, [(xin[lo - ta:hi - ta, j, :], gx[lo:hi, :])], d_xt[j], writes=[r_xin[j]])
                    if bi == 0:
                        load_x(w0, nt)

                    def proj_residual(ci0, lhs_fn, rl, from_xin=False, norm=True, rl_fn=None):
                        ks = [wload(ci0), wload(ci0 + 1)]
                        for j in range(nt):
                            for ch in range(2):
                                k = ks[ch]
                                w3 = WR[k].rearrange("p (k n) -> p k n", k=8)
                                bk = pbk[(2 * j + ch) % 4]
                                for kc in range(8):
                                    MM(ps[:, bk, :], lhs_fn(kc, j), w3[:, kc, :], kc == 0, kc == 7, (rl_fn(kc) if rl_fn else rl) + [r_WR[k]], [rps[bk]], inc=(kc == 7))
                                src, rsrc = (xin, r_xin[j]) if from_xin else (xt, r_xt[j])
                                TT(S.dve, xt[:, j, ch * 512:(ch + 1) * 512], src[:, j, ch * 512:(ch + 1) * 512], ps[:, bk, :], ALU.add,
                                   [rps[bk], rsrc, r_xt[j]], [r_xt[j]])
                            if norm and j >= 1:
                                rms_T(xt[:, j - 1, :], r_xt[j - 1], hT, r_hT, 128 * (j - 1), 0, S.dve)
                        if norm:
                            rms_T(xt[:, nt - 1, :], r_xt[nt - 1], hT, r_hT, 128 * (nt - 1), 0, S.dve)

                    proj_residual(0, lambda kc, j: (AT[:, kc, c1 + 128 * j:c1 + 128 * (j + 1)] if kc < 4
                                                    else BT[:, kc - 4, c1 + 128 * j:c1 + 128 * (j + 1)]), [r_AT, r_BT], from_xin=True)
                    if bi + 1 < len(blocks):
                        load_x(blocks[bi + 1][0], blocks[bi + 1][1])
                    wkq = {}

                    def x_qa(hm):
                        cq, hh = hm // 2, hm % 2
                        if hh == 0:
                            wkq[cq] = wload(2 + cq)
                        k = wkq[cq]
                        w3 = WR[k].rearrange("p (k n) -> p k n", k=8)
                        qb = qm2[hm % 2]
                        for dc in range(2):
                            bk = pbk[dc]
                            for kc in range(8):
                                MM(ps[:, bk, 0:N], w3[:, kc, (2 * hh + dc) * 128:(2 * hh + dc + 1) * 128], hT[:, kc, 0:N], kc == 0, kc == 7,
                                   [r_WR[k], r_hT], [rps[bk]], inc=(kc == 7))
                            CP(S.dve, qb[:, dc, 0:N], ps[:, bk, 0:N], [rps[bk]], [r_qm2[hm % 2]])

                    def x_sb(hm):
                        qb = qm2[hm % 2]
                        for kc2 in range(2):
                            bk = pbk[2 + kc2]
                            for dc in range(2):
                                MM(ps[:, bk, 0:N], KmT[:, 2 * hm + dc, kc2 * 128:(kc2 + 1) * 128], qb[:, dc, 0:N], dc == 0, dc == 1,
                                   [r_Km, r_qm2[hm % 2]], [rps[bk]], inc=(dc == 1))
                            ACT(Em[kc2][:, 0:N], ps[:, bk, 0:N], AF.Exp, [rps[bk]], [r_Em[kc2]], scale=1.0 / 16)

                    def x_vc(hm):
                        for kc2 in range(2):
                            MM(ps[:, 7, 0:N], onesb, Em[kc2][:, 0:N], kc2 == 0, kc2 == 1, [r_const, r_Em[kc2]], [rps[7]], inc=(kc2 == 1))
                        ACT(rDm[:, 0:N], ps[:, 7, 0:N], AF.Ln, [rps[7]], [r_rDm])
                        ACT(rDm[:, 0:N], rDm[:, 0:N], AF.Exp, [r_rDm], [r_rDm], scale=-1.0)
                        for dc in range(2):
                            bk = 5 + dc
                            for kc2 in range(2):
                                MM(ps[:, bk, 0:N], Vm[:, kc2, hm * 256 + dc * 128:hm * 256 + (dc + 1) * 128], Em[kc2][:, 0:N], kc2 == 0, kc2 == 1,
                                   [r_Vm, r_Em[kc2]], [rps[bk]], inc=(kc2 == 1))
                            TT(S.dve, omT[:, 2 * hm + dc, 0:N], ps[:, bk, 0:N], rDm[:, 0:N], ALU.mult, [rps[bk], r_rDm], [r_om[2 * hm + dc]])

                    x_qa(0)
                    for hm in range(4):
                        x_sb(hm)
                        if hm + 1 < 4:
                            x_qa(hm + 1)
                        x_vc(hm)
                    proj_residual(4, lambda kc, j: omT[:, kc, 128 * j:128 * (j + 1)], None, rl_fn=lambda kc: [r_om[kc]])
                    if w0 < 0:
                        MSET(S.pool, hT[:, :, 0:1], 0.0, [r_hT])
                    if last and not g["halo"]:
                        MSET(S.pool, hT[:, :, N - 1:N], 0.0, [r_hT])
                    for cu in range(11):
                        k = wload(6 + cu)
                        w4 = WR[k].rearrange("p (a k n) -> p a k n", a=2, k=8)
                        for pp in range(2):
                            cf = 2 * cu + pp
                            bv_, bg_ = pbk[2 * (cf % 2)], pbk[2 * (cf % 2) + 1]
                            accv, accg, tg = accv2[cf % 2], accg2[cf % 2], tg2[cf % 2]
                            r_av, r_ag, r_tg = r_av2[cf % 2], r_ag2[cf % 2], r_tg2[cf % 2]
                            for kc in range(8):
                                MM(ps[:, bv_, 0:N], w4[:, pp, kc, 0:128], hT[:, kc, 0:N], kc == 0, kc == 7, [r_WR[k], r_hT], [rps[bv_]], inc=False)
                            for kc in range(8):
                                MM(ps[:, bg_, 0:N], w4[:, pp, kc, 128:256], hT[:, kc, 0:N], kc == 0, kc == 7, [r_WR[k], r_hT], [rps[bg_]], inc=(kc == 7))
                            for (acc, racc, bk, cc) in ((accv, r_av, bv_, cf), (accg, r_ag, bg_, 22 + cf)):
                                wc = C_FFNW + 3 * cc
                                ACT(acc[:, 1:N - 1], ps[:, bk, 1:N - 1], AF.Identity, [rps[bk], r_const], [racc], scale=col(wc + 1), bias=col(C_FFNB + cc))
                                STT(acc[:, 1:N - 1], ps[:, bk, 0:N - 2], col(wc), acc[:, 1:N - 1], ALU.mult, ALU.add, [rps[bk], racc, r_const], [racc])
                                STT(acc[:, 1:N - 1], ps[:, bk, 2:N], col(wc + 2), acc[:, 1:N - 1], ALU.mult, ALU.add, [rps[bk], racc, r_const], [racc])
                            ACT(tg[:, 1:N - 1], accg[:, 1:N - 1], AF.Tanh, [r_ag], [r_tg], scale=0.5)
                            STT(tg[:, 1:N - 1], tg[:, 1:N - 1], 1.0, accg[:, 1:N - 1], ALU.add, ALU.mult, [r_tg, r_ag], [r_tg])
                            TT(S.pool, gT[:, cf, 1:N - 1], tg[:, 1:N - 1], accv[:, 1:N - 1], ALU.mult, [r_tg, r_av], [r_gT[cf]])
                    def final_tile(j):
                        ky = ycl[0] % 2
                        ycl[0] += 1
                        ACT(junk, xt[:, j, :], AF.Square, [r_xt[j]], [r_fs], accum_out=f_ss)
                        ACT(f_ln, f_ss, AF.Ln, [r_fs, r_const], [r_fs], scale=1.0 / D, bias=epsb)
                        ACT(f_rs, f_ln, AF.Exp, [r_fs], [r_fs], scale=-0.5)
                        STT(xt[:, j, :], xt[:, j, :], f_rs, gfb, ALU.mult, ALU.mult, [r_xt[j], r_fs, r_const], [r_xt[j]])
                        ta = w0 + 128 * j
                        lo = max(tok0, ta + (1 if j == 0 else 0))
                        hi = min(TOWN, ta + 128 - (1 if j == nt - 1 else 0))
                        if hi > lo:
                            S.dma(S.pool, [(g["y"][lo:hi, :], xt[lo - ta:hi - ta, j, :])], d_out[ky], reads=[r_xt[j]], writes=[r_out])

                    for ch in range(2):
                        ks = [wload(17 + 3 * ch + i) for i in range(3)]
                        for j in range(nt):
                            bk = pbk[j]
                            for kc in range(22):
                                k = ks[kc // 8]
                                w3 = WR[k].rearrange("p (k n) -> p k n", k=8)
                                MM(ps[:, bk, :], gT[:, kc, 128 * j:128 * (j + 1)], w3[:, kc % 8, :], kc == 0, kc == 21, [r_gT[kc], r_WR[k]], [rps[bk]], inc=(kc == 21))
                            TT(S.dve, xt[:, j, ch * 512:(ch + 1) * 512], xt[:, j, ch * 512:(ch + 1) * 512], ps[:, bk, :], ALU.add,
                               [rps[bk], r_xt[j]], [r_xt[j]])
                            if ch == 1 and j >= 1:
                                final_tile(j - 1)
                    final_tile(nt - 1)
                barrier()
                checkpoint('phase3')


        except _Stop:
            barrier()
        S.final_wait(S.pool, [r_out])
        for e in S.engines:
            deps = [(d.sem, d.count) for d in d_out if d.count > 0]
            S._wait(e, deps)
        S.emit()
        build.stats = {e.name: (len(e.prog), e.nwaits) for e in S.engines}
        build.stats['arena_hw'] = ar.hw
    return nc


_NC_CACHE = {}


def _rope_table(pos):
    inv = (np.float32(10000.0) ** (-(np.arange(0, 64, 2, dtype=np.float32)) / np.float32(64))).astype(np.float32)
    ang = (pos.astype(np.float32)[:, None] * inv[None, :]).astype(np.float32)
    c, s = np.cos(ang).astype(np.float32), np.sin(ang).astype(np.float32)
    t = np.concatenate([c, s, -s], axis=1)
    return np.ascontiguousarray(t.reshape(-1, 128, 96))


def _cols(p, flip):
    f = lambda a: np.asarray(a, np.float32)
    cw = f(p["conv_dw_w"][0]); fw = f(p["ffn_dw_w"][0])
    if flip:
        cw = cw[::-1]; fw = fw[::-1]
    parts = [
        f(p["norm_mix"][0]).reshape(8, 128).T, f(p["norm_cross"][0]).reshape(8, 128).T,
        f(p["norm_mem"][0]).reshape(8, 128).T, f(p["norm_ffn"][0]).reshape(8, 128).T,
        f(p["conv_norm"][0]).reshape(4, 128).T, f(p["diff_subln"][0]).reshape(128, 1),
        f(p["conv_dw_b"][0]).reshape(4, 128).T,
        cw.reshape(31, 4, 128).transpose(2, 1, 0).reshape(128, 124),
        f(p["ffn_dw_b"][0]).reshape(44, 128).T,
        fw.reshape(3, 44, 128).transpose(2, 1, 0).reshape(128, 132),
    ]
    out = np.zeros((128, NCOL), np.float32)
    c = np.concatenate(parts, axis=1)
    out[:, :c.shape[1]] = c
    return out


def run(inputs, TP, TS, n_cores=8):
    p = inputs
    key = (TP, TS)
    if key not in _NC_CACHE:
        _NC_CACHE[key] = build(TP, TS)
    nc = _NC_CACHE[key]
    f = lambda a: np.ascontiguousarray(np.asarray(a, np.float32))
    xP, xS = f(p["x_prompt"]), f(p["x_sample"])
    mP, mS = f(p["mem_prompt"]), f(p["mem_sample"])
    TSO = TS // 2
    ident = np.eye(128, dtype=np.float32)
    lamv = np.stack([f(p["lambda_q1"][0]), f(p["lambda_k1"][0]), f(p["lambda_q2"][0]), f(p["lambda_k2"][0])])
    shared = dict(ident=ident, lamv=lamv, gfin=f(p["norm_final"]), w_in=f(p["w_in"][0]), w_out=f(p["w_out"][0]),
                  w_mq=f(p["w_mq"][0]), w_mkv=f(p["w_mkv"][0]), w_mo=f(p["w_mo"][0]), w_up=f(p["w_up"][0]), w_down=f(p["w_down"][0]))
    colsv = [_cols(p, False), _cols(p, True)]
    posP = np.arange(TP); posS = np.arange(TS)
    tabs = {0: (_rope_table(posP), _rope_table(posS)), 1: (_rope_table(posP[::-1]), _rope_table(posS[::-1]))}
    in_maps = []
    for c in range(n_cores):
        rev = c % 2
        xp_ = xP[c][::-1] if rev else xP[c]
        xs_ = xS[c // 2][::-1] if rev else xS[c // 2]
        m = dict(shared)
        m.update(xp=np.ascontiguousarray(xp_), xs=np.ascontiguousarray(xs_), memp=mP[c], mems=mS[c // 2],
                 cstp=tabs[rev][0], csts=tabs[rev][1], cols=colsv[rev])
        in_maps.append(m)
    res = run_bass_kernel_spmd(nc, in_maps, core_ids=list(range(n_cores)))
    yP = np.empty((n_cores, TP, D), np.float32)
    yS = np.empty((n_cores // 2, TS, D), np.float32)
    for c in range(n_cores):
        r = res.results[c]
        rev = c % 2
        yP[c] = r["yp"][::-1] if rev else r["yp"]
        if rev:
            yS[c // 2, TSO:] = r["ys"][::-1]
        else:
            yS[c // 2, :TSO] = r["ys"]
    return yP, yS


def kernel(**inputs):
    return run(inputs, 4096, 8192)
```

```python
import contextlib
import numpy as np
import concourse.bass as bass
import concourse.mybir as mybir
from concourse.bass_utils import run_bass_kernel_spmd

F32 = mybir.dt.float32
BF16 = mybir.dt.bfloat16
AF = mybir.ActivationFunctionType
ALU = mybir.AluOpType
AX = mybir.AxisListType


class Res:
    __slots__ = ("name", "w", "r")

    def __init__(self, name=""):
        self.name = name
        self.w = None
        self.r = {}


class DmaSem:
    __slots__ = ("sem", "count")

    def __init__(self, sem):
        self.sem = sem
        self.count = 0


class Eng:
    def __init__(self, name, sem, self_sync):
        self.name = name
        self.sem = sem
        self.self_sync = self_sync
        self.count = 0
        self.seen = {}
        self.prog = []
        self.dangling = False
        self.nwaits = 0


class Sched:
    def __init__(self, nc, stack):
        self.nc = nc
        self.stack = stack
        self.snap = {}
        self.pe = Eng("pe", self.new_sem("s_pe"), False)
        self.act = Eng("act", self.new_sem("s_act"), True)
        self.dve = Eng("dve", self.new_sem("s_dve"), True)
        self.pool = Eng("pool", self.new_sem("s_pool"), True)
        self.sp = Eng("sp", self.new_sem("s_sp"), False)
        self.engines = [self.pe, self.act, self.dve, self.pool, self.sp]

    def new_sem(self, name):
        return self.stack.enter_context(self.nc.semaphore(name))

    def dma_sem(self, name):
        return DmaSem(self.new_sem(name))

    def _deps(self, reads, writes):
        d = []
        for r in reads:
            if r.w is not None:
                d.append(r.w)
        for w in writes:
            if w.w is not None:
                d.append(w.w)
            d.extend(w.r.items())
        return d

    def _wait(self, e, deps):
        need = {}
        for sem, val in deps:
            if need.get(sem, 0) < val:
                need[sem] = val
        for sem, val in need.items():
            if sem is e.sem and not e.self_sync:
                continue
            if e.seen.get(sem, 0) >= val:
                continue
            e.prog.append(lambda h, sem=sem, val=val: h.wait_ge(sem, val))
            e.nwaits += 1
            e.seen[sem] = val
            sn = self.snap.get((sem, val))
            if sn:
                for s2, v2 in sn.items():
                    if e.seen.get(s2, 0) < v2:
                        e.seen[s2] = v2

    def _assign(self, dep, reads, writes):
        sem, val = dep
        for r in reads:
            if r.r.get(sem, 0) < val:
                r.r[sem] = val
        for w in writes:
            w.w = dep
            w.r = {}

    def op(self, e, fn, reads=(), writes=(), inc=True):
        self._wait(e, self._deps(reads, writes))
        if inc:
            e.count += 1
            val = e.count
            sem = e.sem
            e.prog.append(lambda h, fn=fn, sem=sem: fn(h).then_inc(sem, 1))
            self.snap[(sem, val)] = dict(e.seen)
            e.dangling = False
        else:
            val = e.count + 1
            e.prog.append(lambda h, fn=fn: fn(h))
            e.dangling = True
        self._assign((e.sem, val), reads, writes)

    def dma(self, q, pairs, ds, reads=(), writes=()):
        self._wait(q, self._deps(reads, writes))
        for (o, i) in pairs:
            ds.count += 16
            q.prog.append(lambda h, o=o, i=i, sem=ds.sem: h.dma_start(out=o, in_=i).then_inc(sem, 16))
        dep = (ds.sem, ds.count)
        self.snap[dep] = dict(q.seen)
        self._assign(dep, reads, writes)

    def final_wait(self, e, ress):
        d = []
        for r in ress:
            if r.w is not None:
                d.append(r.w)
            d.extend(r.r.items())
        self._wait(e, d)

    def emit(self):
        for e in self.engines:
            assert not e.dangling, e.name
        nc = self.nc
        with nc.Block() as block:
            @block.tensor
            def _(h):
                for f in self.pe.prog:
                    f(h)

            @block.scalar
            def _(h):
                for f in self.act.prog:
                    f(h)

            @block.vector
            def _(h):
                for f in self.dve.prog:
                    f(h)

            @block.gpsimd
            def _(h):
                for f in self.pool.prog:
                    f(h)

            @block.sync
            def _(h):
                for f in self.sp.prog:
                    f(h)


D = 1024
EPS = 1e-6
DFF = 2816
C_GMIX, C_GCROSS, C_GMEM, C_GFFN, C_GCONV, C_GSUB, C_CONVB, C_CONVW, C_FFNB, C_FFNW, NCOL = 0, 8, 16, 24, 32, 36, 37, 41, 165, 209, 344
import os
FORCE_OWN = int(os.environ.get("FORCE_OWN", "1000"))
NWCH = 23


class Arena:
    def __init__(self, nc, st, kib):
        self.t = st.enter_context(nc.sbuf_tensor("arena", [128, kib * 512], BF16))
        self.limit = kib * 1024
        self.top = 0

    def alloc(self, shape, dt):
        n = 1
        for s in shape:
            n *= s
        esz = 4 if dt == F32 else 2
        nb = (n * esz + 31) // 32 * 32
        off = self.top
        assert off + nb <= self.limit, ("SBUF arena overflow", off + nb, self.limit)
        self.top += nb
        self.hw = max(getattr(self, 'hw', 0), self.top)
        a = self.t[:, off // 2: off // 2 + n * esz // 2]
        if dt == F32:
            a = a.bitcast(F32)
        if len(shape) == 2:
            a = a.rearrange("p (a b) -> p a b", a=shape[0])
        elif len(shape) == 3:
            a = a.rearrange("p (a b c) -> p a b c", a=shape[0], b=shape[1])
        return a


class _Stop(Exception):
    pass


def build(TP, TS, stop=None):
    TSO = TS // 2
    stg_i = [1]

    def checkpoint(name):
        stg_i[0] += 1
        if stop is not None and stg_i[0] >= stop:
            print('STOP at', stg_i[0], name)
            raise _Stop()
    nc = bass.Bass("TRN2", target_bir_lowering=False)

    def din(name, shape, dt=F32):
        return nc.dram_tensor(name, shape, dt, kind="ExternalInput").ap()

    xp = din("xp", [TP, D]); xs = din("xs", [TS, D])
    memp = din("memp", [256, D]); mems = din("mems", [256, D])
    cstp = din("cstp", [TP // 128, 128, 96]); csts = din("csts", [TS // 128, 128, 96])
    identd = din("ident", [128, 128])
    cols = din("cols", [128, NCOL])
    lamv = din("lamv", [4, 64])
    gfin = din("gfin", [D])
    w_in = din("w_in", [D, 2560]); w_out = din("w_out", [D, D]); w_mq = din("w_mq", [D, D])
    w_mkv = din("w_mkv", [D, 2 * D]); w_mo = din("w_mo", [D, D])
    w_up = din("w_up", [D, 2 * DFF]); w_down = din("w_down", [DFF, D])
    yp = nc.dram_tensor("yp", [TP, D], F32, kind="ExternalOutput").ap()
    ys = nc.dram_tensor("ys", [TSO, D], F32, kind="ExternalOutput").ap()
    wscr = nc.dram_tensor("wscr", [NWCH, 128, 4096], BF16, kind="Internal").ap()

    with contextlib.ExitStack() as st:
        S = Sched(nc, st)
        S.dsems = []

        def dsem(name):
            d = S.dma_sem(name)
            S.dsems.append(d)
            return d

        def barrier():
            for e in S.engines:
                assert not e.dangling
            for e in S.engines:
                deps = [(o.sem, o.count) for o in S.engines if o is not e and o.count > 0]
                deps += [(d.sem, d.count) for d in S.dsems if d.count > 0]
                S._wait(e, deps)

        ar = Arena(nc, st, 206)
        ps = st.enter_context(nc.psum_tensor("ps", [128, 8, 512], F32))
        rps = [Res("ps%d" % i) for i in range(8)]

        def psb(i):
            return ps[:, i, :].bitcast(BF16)

        def ACT(out, in_, func, reads, writes, **kw):
            S.op(S.act, lambda h: h.activation(out=out, in_=in_, func=func, **kw), reads, writes)

        def MM(out, lhsT, rhs, start, stop, reads, writes, inc=True):
            S.op(S.pe, lambda h: h.matmul(out, lhsT=lhsT, rhs=rhs, start=start, stop=stop), reads, writes, inc)

        def TR(out, in_, reads, writes, inc=True):
            S.op(S.pe, lambda h: h.transpose(out=out, in_=in_, identity=identb), reads, writes, inc)

        def TSC(eng, out, in0, s1, s2, op0, op1, reads, writes):
            if s2 is None:
                S.op(eng, lambda h: h.tensor_scalar(out=out, in0=in0, scalar1=s1, scalar2=None, op0=op0), reads, writes)
            else:
                S.op(eng, lambda h: h.tensor_scalar(out=out, in0=in0, scalar1=s1, scalar2=s2, op0=op0, op1=op1), reads, writes)

        def TT(eng, out, in0, in1, op, reads, writes):
            S.op(eng, lambda h: h.tensor_tensor(out=out, in0=in0, in1=in1, op=op), reads, writes)

        def STT(out, in0, scalar, in1, op0, op1, reads, writes):
            S.op(S.dve, lambda h: h.scalar_tensor_tensor(out=out, in0=in0, scalar=scalar, in1=in1, op0=op0, op1=op1), reads, writes)

        def CP(eng, out, in_, reads, writes):
            if eng is S.act:
                ACT(out, in_, AF.Copy, reads, writes)
            else:
                S.op(eng, lambda h: h.tensor_copy(out=out, in_=in_), reads, writes)

        def MSET(eng, ap, val, writes):
            S.op(eng, lambda h: h.memset(ap, val), (), writes)

        def RECIP(out, in_, reads, writes):
            S.op(S.dve, lambda h: h.reciprocal(out=out, in_=in_), reads, writes)

        identf = ar.alloc([128], F32); identb = ar.alloc([128], BF16); onesb = ar.alloc([128], BF16)
        onesf = ar.alloc([128], F32)
        colt = ar.alloc([NCOL], F32)
        epsb = ar.alloc([1], F32); nlam = ar.alloc([1], F32)
        gfb = ar.alloc([D], F32)
        r_const = Res("const")
        d_const = dsem("d_const")
        S.dma(S.sp, [(identf, identd), (colt, cols), (gfb, gfin.partition_broadcast(128))], d_const, writes=[r_const])
        CP(S.dve, identb, identf, [r_const], [r_const])
        MSET(S.pool, onesb, 1.0, [r_const])
        MSET(S.pool, onesf, 1.0, [r_const])
        MSET(S.pool, epsb, EPS, [r_const])
        LAM_INIT = 0.2
        mk0 = ar.top
        lv = ar.alloc([4, 64], F32); lm = ar.alloc([2, 64], F32); lsum = ar.alloc([2], F32); lexp = ar.alloc([2], F32)
        r_l = Res("lam")
        S.dma(S.sp, [(lv[:, i, :], lamv[i].partition_broadcast(128)) for i in range(4)], d_const, writes=[r_l])
        TT(S.dve, lm[:, 0, :], lv[:, 0, :], lv[:, 1, :], ALU.mult, [r_l], [r_l])
        TT(S.dve, lm[:, 1, :], lv[:, 2, :], lv[:, 3, :], ALU.mult, [r_l], [r_l])
        for i in range(2):
            ACT(lv[:, i, :], lm[:, i, :], AF.Copy, [r_l], [r_l], accum_out=lsum[:, i:i + 1])
        ACT(lexp, lsum, AF.Exp, [r_l], [r_l])
        TT(S.dve, lexp[:, 0:1], lexp[:, 0:1], lexp[:, 1:2], ALU.subtract, [r_l], [r_l])
        TSC(S.dve, nlam, lexp[:, 0:1], -1.0, -LAM_INIT, ALU.mult, ALU.add, [r_l], [r_const])
        TSC(S.dve, colt[:, C_GSUB:C_GSUB + 1], colt[:, C_GSUB:C_GSUB + 1], 1.0 - LAM_INIT, None, ALU.mult, None, [r_const], [r_const])
        TSC(S.dve, colt[:, C_CONVW:C_CONVW + 124], colt[:, C_CONVW:C_CONVW + 124], 0.5, None, ALU.mult, None, [r_const], [r_const])
        barrier()
        ar.top = mk0
        base_top = ar.top
        setup_stop = (stop == 1)

        def col(c0, n=1):
            return colt[:, c0:c0 + n]

        r_out = Res("out")
        d_out = [dsem("d_out%d" % i) for i in range(2)]
        try:
            if setup_stop:
                raise _Stop()
            cv_cnt = [0]

            def convert(out, in_, scale_ap, const, reads, writes):
                cv_cnt[0] += 1
                if cv_cnt[0] % 2 == 0:
                    if scale_ap is None:
                        TSC(S.dve, out, in_, const, None, ALU.mult, None, reads, writes)
                    elif const == 1.0:
                        TSC(S.dve, out, in_, scale_ap, None, ALU.mult, None, reads, writes)
                    else:
                        raise AssertionError('mixed scale')
                else:
                    if scale_ap is None:
                        ACT(out, in_, AF.Copy, reads, writes, scale=const)
                    elif const == 1.0:
                        ACT(out, in_, AF.Identity, reads, writes, scale=scale_ap)
                    else:
                        raise AssertionError('mixed scale')

            r_wscr = Res("wscr")
            stg = [ar.alloc([4096], F32) for _ in range(2)]
            cvb = [ar.alloc([4096], BF16) for _ in range(2)]
            r_stg = [Res("stg%d" % i) for i in range(2)]
            r_cvb = [Res("cvb%d" % i) for i in range(2)]
            d_stg = [dsem("d_stg%d" % i) for i in range(2)]
            d_cvb = [dsem("d_cvb%d" % i) for i in range(2)]

            def kview(w, r0, nk, c0, nc_):
                return w[r0:r0 + nk * 128, c0:c0 + nc_].rearrange("(k p) n -> p k n", p=128)

            for ci in range(NWCH):
                k = ci % 2
                sv, cvv = stg[k], cvb[k]
                if ci < 6:
                    wsrc = [w_out, w_mq, w_mo][ci // 2]
                    ch = ci % 2
                    s3 = sv.rearrange("p (k n) -> p k n", k=8)
                    c3 = cvv.rearrange("p (k n) -> p k n", k=8)
                    S.dma(S.sp, [(s3, kview(wsrc, 0, 8, ch * 512, 512))], d_stg[k], writes=[r_stg[k]])
                    for kc in range(8):
                        if ci < 2:
                            sc, cst = (None, 0.5) if kc < 4 else (col(C_GSUB), 1.0)
                        elif ci < 4:
                            sc, cst = col(C_GCROSS + kc), 1.0
                        else:
                            sc, cst = None, 1.0
                        convert(c3[:, kc, :], s3[:, kc, :], sc, cst, [r_stg[k], r_const], [r_cvb[k]])
                elif ci < 17:
                    cu = ci - 6
                    s4 = sv.rearrange("p (a k n) -> p a k n", a=2, k=8)
                    c4 = cvv.rearrange("p (a k n) -> p a k n", a=2, k=8)
                    pairs = []
                    for pp in range(2):
                        cf = 2 * cu + pp
                        pairs.append((s4[:, pp, :, 0:128], kview(w_up, 0, 8, cf * 128, 128)))
                        pairs.append((s4[:, pp, :, 128:256], kview(w_up, 0, 8, DFF + cf * 128, 128)))
                    S.dma(S.sp, pairs, d_stg[k], writes=[r_stg[k]])
                    for kc in range(8):
                        convert(c4[:, :, kc, :], s4[:, :, kc, :], col(C_GFFN + kc), 1.0, [r_stg[k], r_const], [r_cvb[k]])
                else:
                    j = ci - 17
                    ch, kci = j // 3, j % 3
                    nk = 8 if kci < 2 else 6
                    s3 = sv.rearrange("p (k n) -> p k n", k=8)
                    c3 = cvv.rearrange("p (k n) -> p k n", k=8)
                    S.dma(S.sp, [(s3[:, 0:nk, :], kview(w_down, kci * 1024, nk, ch * 512, 512))], d_stg[k], writes=[r_stg[k]])
                    convert(c3[:, 0:nk, :], s3[:, 0:nk, :], None, 0.5, [r_stg[k]], [r_cvb[k]])
                S.dma(S.pool, [(wscr[ci], cvv)], d_cvb[k], reads=[r_cvb[k]], writes=[r_wscr])
            barrier()
            checkpoint('prepass')
            ar.top = base_top

            junk = ar.alloc([D], BF16)
            n_ss = [ar.alloc([1], F32) for _ in range(4)]
            n_ln = [ar.alloc([1], F32) for _ in range(4)]
            n_rs = [ar.alloc([1], F32) for _ in range(4)]
            n_xn = [ar.alloc([D], BF16) for _ in range(3)]
            r_ss = [Res() for _ in range(4)]; r_ln = [Res() for _ in range(4)]
            r_rs = [Res() for _ in range(4)]; r_xn = [Res() for _ in range(3)]
            ncnt = [0]

            def rms_stats(x_ap, rx, k):
                ACT(junk, x_ap, AF.Square, [rx], [r_ss[k]], accum_out=n_ss[k])
                ACT(n_ln[k], n_ss[k], AF.Ln, [r_ss[k], r_const], [r_ln[k]], scale=1.0 / D, bias=epsb)
                ACT(n_rs[k], n_ln[k], AF.Exp, [r_ln[k]], [r_rs[k]], scale=-0.5)

            def rms_scale(x_ap, rx, k, kx):
                TSC(S.dve, n_xn[kx], x_ap, n_rs[k], None, ALU.mult, None, [rx, r_rs[k]], [r_xn[kx]])

            def rms_tr(kx, trbank):
                pb = psb(trbank)
                for c in range(8):
                    TR(pb[:, c * 128:(c + 1) * 128], n_xn[kx][:, c * 128:(c + 1) * 128], [r_xn[kx], r_const], [rps[trbank]], inc=(c == 7))

            def rms_evac(evac, trbank, dstT, rdst, col0):
                CP(evac, dstT[:, :, col0:col0 + 128], psb(trbank).rearrange("p (a b) -> p a b", a=8), [rps[trbank]], [rdst])

            def rms_T(x_ap, rx, dstT, rdst, col0, trbank, evac):
                k, kx = ncnt[0] % 4, ncnt[0] % 3
                ncnt[0] += 1
                rms_stats(x_ap, rx, k)
                rms_scale(x_ap, rx, k, kx)
                rms_tr(kx, trbank)
                rms_evac(evac, trbank, dstT, rdst, col0)

            def run_pipeline(stages, n):
                lo = -max(o for _, o in stages)
                hi = n - min(o for _, o in stages)
                for i in range(lo, hi):
                    for fn, off in stages:
                        if 0 <= i + off < n:
                            fn(i + off)

            norm_top = ar.top

            groups = [
                dict(name="p", x=xp, y=yp, TK=TP, TOWN=TP, NQ=TP // 128, halo=False, cst=cstp, mem=memp),
                dict(name="s", x=xs, y=ys, TK=TS, TOWN=TSO, NQ=TSO // 128 + 1, halo=True, cst=csts, mem=mems),
            ]
            d_x = [dsem("d_x%d" % i) for i in range(5)]
            d_xt = [dsem("d_xt%d" % i) for i in range(4)]
            d_cs = [dsem("d_cs%d" % i) for i in range(5)]
            d_w = [dsem("d_w%d" % i) for i in range(3)]
            d_sg = dsem("d_sg")

            for g in groups:
                ar.top = norm_top
                gx, TK, TOWN, NQ = g["x"], g["TK"], g["TOWN"], g["NQ"]
                NK = TK // 128
                TQ = NQ * 128
                BT = ar.alloc([4, TQ + 2], BF16)
                r_AT = Res("AT"); r_BT = Res("BT")
                KmT = ar.alloc([8, 256], BF16); Vm = ar.alloc([2, D], BF16)
                r_Km = Res("KmT"); r_Vm = Res("Vm")
                MSET(S.pool, BT[:, :, 0:1], 0.0, [r_BT]); MSET(S.pool, BT[:, :, TQ + 1:TQ + 2], 0.0, [r_BT])
                grp_top = ar.top

                xr = [ar.alloc([D], F32) for _ in range(3)]
                r_xr = [Res() for _ in range(3)]
                memT = ar.alloc([8, 256], BF16); r_memT = Res()
                sg = ar.alloc([8, 256], F32); r_sg = Res()
                wp = ar.alloc([8, 256], BF16); r_wp = Res()
                for t in range(2):
                    S.dma(S.sp, [(xr[t], g["mem"][t * 128:(t + 1) * 128, :])], d_x[t], writes=[r_xr[t]])
                    rms_T(xr[t], r_xr[t], memT, r_memT, t * 128, t, S.act)
                for j in range(8):
                    S.dma(S.sp, [(sg, kview(w_mkv, 0, 8, j * 256, 256))], d_sg, writes=[r_sg])
                    for kc in range(8):
                        convert(wp[:, kc, :], sg[:, kc, :], col(C_GMEM + kc), 1.0, [r_sg, r_const], [r_wp])
                    if j < 4:
                        for f2 in range(2):
                            fc = 2 * j + f2
                            bk = 2 + (fc % 2)
                            for kc in range(8):
                                MM(ps[:, bk, 0:256], wp[:, kc, f2 * 128:(f2 + 1) * 128], memT[:, kc, :], kc == 0, kc == 7,
                                   [r_wp, r_memT], [rps[bk]], inc=(kc == 7))
                            CP(S.act, KmT[:, fc, :], ps[:, bk, 0:256], [rps[bk]], [r_Km])
                    else:
                        for kc2 in range(2):
                            bk = 2 + kc2
                            for kc in range(8):
                                MM(ps[:, bk, 0:256], memT[:, kc, kc2 * 128:(kc2 + 1) * 128], wp[:, kc, :], kc == 0, kc == 7,
                                   [r_wp, r_memT], [rps[bk]], inc=(kc == 7))
                            CP(S.dve, Vm[:, kc2, (j - 4) * 256:(j - 3) * 256], ps[:, bk, 0:256], [rps[bk]], [r_Vm])
                barrier()
                checkpoint('stageM')
                ar.top = grp_top

                for hp in range(2):
                    ar.top = grp_top
                    KT = ar.alloc([2, TK], BF16); VV = ar.alloc([NK, 256], BF16); QT = ar.alloc([2, TQ], BF16)
                    r_KT = Res("KT"); r_VV = Res("V"); r_QT = Res("QT")
                    passA_top = ar.top
                    Wq = ar.alloc([8, 768], BF16); r_Wq = Res("Wq")
                    sg = ar.alloc([8, 256], F32); r_sg = Res()
                    for pc in range(3):
                        c0 = 1024 + 512 * pc + 256 * hp
                        S.dma(S.sp, [(sg, kview(w_in, 0, 8, c0, 256))], d_sg, writes=[r_sg])
                        for kc in range(8):
                            convert(Wq[:, kc, pc * 256:(pc + 1) * 256], sg[:, kc, :], col(C_GMIX + kc), 1.0, [r_sg, r_const], [r_Wq])
                    xr = [ar.alloc([D], F32) for _ in range(5)]
                    r_xr = [Res() for _ in range(5)]
                    csr = [ar.alloc([3, 32], F32) for _ in range(5)]
                    r_cs = [Res() for _ in range(5)]
                    xnT = [ar.alloc([8, 128], BF16) for _ in range(2)]
                    r_xnT = [Res() for _ in range(2)]
                    tA = [ar.alloc([512], F32) for _ in range(2)]; tB = [ar.alloc([512], F32) for _ in range(2)]
                    rk = [ar.alloc([512], BF16) for _ in range(2)]
                    r_tA = [Res() for _ in range(2)]; r_tB = [Res() for _ in range(2)]; r_rk = [Res() for _ in range(2)]
                    def s_load(t):
                        k5 = t % 5
                        S.dma(S.sp, [(xr[k5], gx[t * 128:(t + 1) * 128, :])], d_x[k5], writes=[r_xr[k5]])
                        S.dma(S.sp, [(csr[k5], g["cst"][t].rearrange("p (a f) -> p a f", a=3))], d_cs[k5], writes=[r_cs[k5]])

                    def s_stats(t):
                        rms_stats(xr[t % 5], r_xr[t % 5], t % 4)

                    def s_scale(t):
                        rms_scale(xr[t % 5], r_xr[t % 5], t % 4, t % 3)

                    def s_tr(t):
                        rms_tr(t % 3, t % 2)

                    def s_evac(t):
                        rms_evac(S.act, t % 2, xnT[t % 2], r_xnT[t % 2], 0)

                    def s_mm(t):
                        k2 = t % 2
                        bA, bB = 2 + k2, 4 + k2
                        for kc in range(8):
                            MM(ps[:, bA, :], xnT[k2][:, kc, :], Wq[:, kc, 0:512], kc == 0, kc == 7, [r_xnT[k2], r_Wq], [rps[bA]], inc=False)
                        for kc in range(8):
                            MM(ps[:, bB, 0:256], xnT[k2][:, kc, :], Wq[:, kc, 512:768], kc == 0, kc == 7, [r_xnT[k2], r_Wq], [rps[bB]], inc=(kc == 7))

                    def s_vcopy(t):
                        CP(S.act, VV[:, t, :], ps[:, 4 + t % 2, 0:256], [rps[4 + t % 2]], [r_VV])

                    def s_rope(t):
                        k2, k5 = t % 2, t % 5
                        bA = 2 + k2
                        xv = ps[:, bA, :].rearrange("p (a b f) -> p a b f", a=8, b=2)
                        av = tA[k2].rearrange("p (a b f) -> p a b f", a=8, b=2)
                        bv = tB[k2].rearrange("p (a b f) -> p a b f", a=8, b=2)
                        cosb = csr[k5][:, 0, :].unsqueeze(1).unsqueeze(1).to_broadcast([128, 8, 2, 32])
                        sinb = csr[k5][:, 1, :].unsqueeze(1).to_broadcast([128, 8, 32])
                        nsinb = csr[k5][:, 2, :].unsqueeze(1).to_broadcast([128, 8, 32])
                        TT(S.dve, av, xv, cosb, ALU.mult, [rps[bA], r_cs[k5]], [r_tA[k2]])
                        TT(S.dve, bv[:, :, 0, :], xv[:, :, 1, :], nsinb, ALU.mult, [rps[bA], r_cs[k5]], [r_tB[k2]])
                        TT(S.dve, bv[:, :, 1, :], xv[:, :, 0, :], sinb, ALU.mult, [rps[bA], r_cs[k5]], [r_tB[k2]])

                    def s_add(t):
                        k2 = t % 2
                        TT(S.pool, rk[k2], tA[k2], tB[k2], ALU.add, [r_tA[k2], r_tB[k2]], [r_rk[k2]])

                    def s_rktr(t):
                        k2 = t % 2
                        pbt = psb(6 + k2)
                        for c in range(4):
                            TR(pbt[:, c * 128:(c + 1) * 128], rk[k2][:, c * 128:(c + 1) * 128], [r_rk[k2], r_const], [rps[6 + k2]], inc=(c == 3))

                    def s_copies(t):
                        k2 = t % 2
                        p3 = psb(6 + k2)[:, 0:512].rearrange("p (a b) -> p a b", a=4)
                        if t < NQ:
                            CP(S.dve, QT[:, :, t * 128:(t + 1) * 128], p3[:, 0:2, :], [rps[6 + k2]], [r_QT])
                        CP(S.dve, KT[:, :, t * 128:(t + 1) * 128], p3[:, 2:4, :], [rps[6 + k2]], [r_KT])

                    run_pipeline([(s_load, 3), (s_stats, 3), (s_scale, 2), (s_tr, 1), (s_evac, 1), (s_mm, 0), (s_vcopy, -1),
                                  (s_rope, -1), (s_add, -1), (s_rktr, -2), (s_copies, -3)], NK)
                    barrier()
                    checkpoint('stageA')
                    ar.top = passA_top
                    NEB = 4
                    Eb = [ar.alloc([2, 512], BF16) for _ in range(NEB)]
                    r_E = [Res() for _ in range(NEB)]
                    Esd = ar.alloc([2, 512], F32)
                    r_Esd = Res()
                    rD = [ar.alloc([512], F32) for _ in range(2)]; r_rD = [Res() for _ in range(2)]
                    t0 = ar.alloc([512], F32); r_t0 = Res()
                    Rr = ar.alloc([512], F32); r_R = Res()
                    Rsq = ar.alloc([512], BF16); r_Rsq = Res()
                    lnv = ar.alloc([512], F32); r_lnv = Res()
                    rsd = ar.alloc([512], F32); r_rsd = Res()
                    qgs = [(q0, min(512, TQ - q0)) for q0 in range(0, TQ, 512)]
                    ecnt = 0
                    pending = [None]

                    def make_deferred(h_, q0_, nq_):
                        def d():
                            MM(ps[:, 6, 0:nq_], onesb, Rsq[:, 0:nq_], True, True, [r_const, r_Rsq], [rps[6]])
                            ACT(lnv[:, 0:nq_], ps[:, 6, 0:nq_], AF.Ln, [rps[6], r_const], [r_lnv], scale=1.0 / 128, bias=epsb)
                            ACT(rsd[:, 0:nq_], lnv[:, 0:nq_], AF.Exp, [r_lnv], [r_rsd], scale=-0.5)
                            TT(S.dve, BT[:, h_, 1 + q0_:1 + q0_ + nq_], Rr[:, 0:nq_], rsd[:, 0:nq_], ALU.mult, [r_R, r_rsd], [r_BT])
                        return d
                    for hl in range(2):
                        h = 2 * hp + hl
                        for (q0, nq) in qgs:
                            def qk(kt, sb_):
                                b0 = 2 * sb_
                                MM(ps[:, b0, 0:nq], KT[0:64, hl, kt * 128:(kt + 1) * 128], QT[0:64, hl, q0:q0 + nq], True, True,
                                   [r_KT, r_QT], [rps[b0]], inc=False)
                                MM(ps[:, b0 + 1, 0:nq], KT[64:128, hl, kt * 128:(kt + 1) * 128], QT[64:128, hl, q0:q0 + nq], True, True,
                                   [r_KT, r_QT], [rps[b0 + 1]], inc=True)
                            qk(0, 0)
                            for kt in range(NK):
                                sb_ = kt % 2
                                b0 = 2 * sb_
                                e = ecnt % NEB
                                ecnt += 1
                                ACT(Eb[e][:, :, 0:nq], ps[:, b0:b0 + 2, 0:nq], AF.Exp, [rps[b0], rps[b0 + 1]], [r_E[e]], scale=0.125)
                                if kt + 1 < NK:
                                    qk(kt + 1, 1 - sb_)
                                st_, sp_ = (kt == 0), (kt == NK - 1)
                                vv = VV[:, kt, hl * 128:(hl + 1) * 128]
                                MM(ps[:, 4, 0:nq], vv, Eb[e][:, 0, 0:nq], st_, sp_, [r_VV, r_E[e]], [rps[4]], inc=False)
                                MM(ps[:, 5, 0:nq], vv, Eb[e][:, 1, 0:nq], st_, sp_, [r_VV, r_E[e]], [rps[5]], inc=True)
                                if kt == 5 and pending[0] is not None:
                                    pending[0]()
                                    pending[0] = None
                                if kt % 8 == 7:
                                    MM(ps[:, 6, 0:nq], onesb, Eb[e][:, 0, 0:nq], kt == 7, False, [r_const, r_E[e]], [rps[6]], inc=False)
                                    MM(ps[:, 7, 0:nq], onesb, Eb[e][:, 1, 0:nq], kt == 7, False, [r_const, r_E[e]], [rps[7]], inc=True)
                                elif kt == 0:
                                    CP(S.dve, Esd[:, :, 0:nq], Eb[e][:, :, 0:nq], [r_E[e]], [r_Esd])
                                else:
                                    TT(S.dve, Esd[:, :, 0:nq], Esd[:, :, 0:nq], Eb[e][:, :, 0:nq], ALU.add, [r_E[e], r_Esd], [r_Esd])
                            MM(ps[:, 6, 0:nq], onesf, Esd[:, 0, 0:nq], NK < 8, True, [r_const, r_Esd], [rps[6]], inc=False)
                            MM(ps[:, 7, 0:nq], onesf, Esd[:, 1, 0:nq], NK < 8, True, [r_const, r_Esd], [rps[7]], inc=True)
                            ACT(rD[0][:, 0:nq], ps[:, 6, 0:nq], AF.Ln, [rps[6]], [r_rD[0]])
                            ACT(rD[1][:, 0:nq], ps[:, 7, 0:nq], AF.Ln, [rps[7]], [r_rD[1]])
                            CP(S.dve, t0[:, 0:nq], ps[:, 4, 0:nq], [rps[4]], [r_t0])
                            CP(S.dve, Rr[:, 0:nq], ps[:, 5, 0:nq], [rps[5]], [r_R])
                            ACT(rD[0][:, 0:nq], rD[0][:, 0:nq], AF.Exp, [r_rD[0]], [r_rD[0]], scale=-1.0)
                            ACT(rD[1][:, 0:nq], rD[1][:, 0:nq], AF.Exp, [r_rD[1]], [r_rD[1]], scale=-1.0)
                            TT(S.dve, t0[:, 0:nq], t0[:, 0:nq], rD[0][:, 0:nq], ALU.mult, [r_t0, r_rD[0]], [r_t0])
                            TT(S.dve, Rr[:, 0:nq], Rr[:, 0:nq], rD[1][:, 0:nq], ALU.mult, [r_R, r_rD[1]], [r_R])
                            STT(Rr[:, 0:nq], Rr[:, 0:nq], nlam, t0[:, 0:nq], ALU.mult, ALU.add, [r_R, r_t0, r_const], [r_R])
                            TT(S.dve, Rsq[:, 0:nq], Rr[:, 0:nq], Rr[:, 0:nq], ALU.mult, [r_R], [r_Rsq])
                            pending[0] = make_deferred(h, q0, nq)
                    if pending[0] is not None:
                        pending[0]()
                        pending[0] = None
                    barrier()
                    checkpoint('stageB')

                ar.top = grp_top
                AT = ar.alloc([4, TQ + 2], BF16)
                MSET(S.pool, AT[:, :, 0:1], 0.0, [r_AT]); MSET(S.pool, AT[:, :, TQ + 1:TQ + 2], 0.0, [r_AT])
                at_top = ar.top
                Dg = ar.alloc([124, 128], BF16); r_Dg = Res("Dg")
                for c in range(4):
                    for k in range(31):
                        TSC(S.dve, Dg[:, c * 31 + k, :], identf, col(C_CONVW + c * 31 + k), None, ALU.mult, None, [r_const], [r_Dg])
                GW = TQ + 30
                G = ar.alloc([4, GW], BF16); r_G = Res("G")
                MSET(S.pool, G[:, :, 0:15], 0.0, [r_G]); MSET(S.pool, G[:, :, 15 + TQ:GW], 0.0, [r_G])
                c0_top = ar.top
                Wa = ar.alloc([8, 1024], BF16); r_Wa = Res("Wa")
                sg = ar.alloc([8, 256], F32); r_sg = Res()
                for pc in range(4):
                    S.dma(S.sp, [(sg, kview(w_in, 0, 8, pc * 256, 256))], d_sg, writes=[r_sg])
                    for kc in range(8):
                        convert(Wa[:, kc, pc * 256:(pc + 1) * 256], sg[:, kc, :], col(C_GMIX + kc), 1.0, [r_sg, r_const], [r_Wa])
                xr = [ar.alloc([D], F32) for _ in range(5)]
                r_xr = [Res() for _ in range(5)]
                xnT = [ar.alloc([8, 128], BF16) for _ in range(2)]
                r_xnT = [Res() for _ in range(2)]
                th = [ar.alloc([512], F32) for _ in range(2)]; r_th = [Res() for _ in range(2)]
                def c_load(t):
                    S.dma(S.sp, [(xr[t % 5], gx[t * 128:(t + 1) * 128, :])], d_x[t % 5], writes=[r_xr[t % 5]])

                def c_stats(t):
                    rms_stats(xr[t % 5], r_xr[t % 5], t % 4)

                def c_scale(t):
                    rms_scale(xr[t % 5], r_xr[t % 5], t % 4, t % 3)

                def c_tr(t):
                    rms_tr(t % 3, t % 2)

                def c_evac(t):
                    rms_evac(S.act, t % 2, xnT[t % 2], r_xnT[t % 2], 0)

                def c_mm(t):
                    k2 = t % 2
                    bV, bG = 2 + k2, 4 + k2
                    for fc in range(8):
                        bk = bV if fc < 4 else bG
                        for kc in range(8):
                            MM(ps[:, bk, (fc % 4) * 128:(fc % 4 + 1) * 128], Wa[:, kc, fc * 128:(fc + 1) * 128], xnT[k2][:, kc, :],
                               kc == 0, kc == 7, [r_Wa, r_xnT[k2]], [rps[bk]], inc=(kc == 7 and fc % 4 == 3))

                def c_glu(t):
                    k2 = t % 2
                    bV, bG = 2 + k2, 4 + k2
                    ACT(th[k2], ps[:, bG, :], AF.Tanh, [rps[bG]], [r_th[k2]], scale=0.5)
                    STT(G[:, :, 15 + t * 128:15 + (t + 1) * 128], th[k2].rearrange("p (a b) -> p a b", a=4), 1.0,
                        ps[:, bV, :].rearrange("p (a b) -> p a b", a=4), ALU.add, ALU.mult, [r_th[k2], rps[bV]], [r_G])

                run_pipeline([(c_load, 3), (c_stats, 3), (c_scale, 2), (c_tr, 1), (c_evac, 1), (c_mm, 0), (c_glu, -1)], NQ)
                barrier()
                ar.top = c0_top
                cvb_ = ar.alloc([4, 512], F32); r_cv = Res()
                sq = ar.alloc([4, 512], BF16); r_sq = Res()
                lnv = ar.alloc([512], F32); r_lnv = Res()
                rsd = ar.alloc([512], F32); r_rsd = Res()
                th2 = ar.alloc([4, 512], F32); r_th2 = Res()
                for c0 in range(0, TQ, 512):
                    n = min(512, TQ - c0)
                    for c in range(4):
                        for k in range(31):
                            MM(ps[:, 4 + c, 0:n], Dg[:, c * 31 + k, :], G[:, c, c0 + k:c0 + k + n], k == 0, k == 30,
                               [r_Dg, r_G], [rps[4 + c]], inc=(k == 30))
                    for c in range(4):
                        ACT(cvb_[:, c, 0:n], ps[:, 4 + c, 0:n], AF.Identity, [rps[4 + c], r_const], [r_cv], bias=col(C_CONVB + c))
                        ACT(sq[:, c, 0:n], ps[:, 4 + c, 0:n], AF.Square, [rps[4 + c], r_const], [r_sq], bias=col(C_CONVB + c))
                    for c in range(4):
                        MM(ps[:, 2, 0:n], onesb, sq[:, c, 0:n], c == 0, c == 3, [r_const, r_sq], [rps[2]], inc=(c == 3))
                    ACT(lnv[:, 0:n], ps[:, 2, 0:n], AF.Ln, [rps[2], r_const], [r_lnv], scale=1.0 / 512, bias=epsb)
                    ACT(rsd[:, 0:n], lnv[:, 0:n], AF.Exp, [r_lnv], [r_rsd], scale=-0.5)
                    for c in range(4):
                        STT(cvb_[:, c, 0:n], cvb_[:, c, 0:n], col(C_GCONV + c), rsd[:, 0:n], ALU.mult, ALU.mult, [r_cv, r_rsd, r_const], [r_cv])
                    ACT(th2[:, :, 0:n], cvb_[:, :, 0:n], AF.Tanh, [r_cv], [r_th2], scale=0.5)
                    STT(AT[:, :, 1 + c0:1 + c0 + n], th2[:, :, 0:n], 1.0, cvb_[:, :, 0:n], ALU.add, ALU.mult, [r_th2, r_cv], [r_AT])
                barrier()
                checkpoint('C0')

                ar.top = at_top
                NB = TOWN // 510
                blocks = [(510 * b - 1, 4, 510 * b) for b in range(NB)]
                blocks.append((TOWN - 127, 1, 510 * NB))
                xt = ar.alloc([4, D], F32); r_xt = [Res() for _ in range(4)]
                xin = ar.alloc([4, D], F32); r_xin = [Res() for _ in range(4)]
                WR = [ar.alloc([4096], BF16) for _ in range(3)]; r_WR = [Res() for _ in range(3)]
                hT = ar.alloc([8, 512], BF16); r_hT = Res("hT")
                qm2 = [ar.alloc([2, 512], BF16) for _ in range(2)]; r_qm2 = [Res(), Res()]
                omT = ar.alloc([8, 512], BF16); r_om = [Res() for _ in range(8)]
                Em = [ar.alloc([512], BF16) for _ in range(2)]; r_Em = [Res() for _ in range(2)]
                rDm = ar.alloc([512], F32); r_rDm = Res()
                gT = ar.alloc([22, 512], BF16); r_gT = [Res("gT%d" % i) for i in range(22)]
                accv2 = [ar.alloc([512], F32) for _ in range(2)]; accg2 = [ar.alloc([512], F32) for _ in range(2)]
                tg2 = [ar.alloc([512], F32) for _ in range(2)]
                r_av2 = [Res() for _ in range(2)]; r_ag2 = [Res() for _ in range(2)]; r_tg2 = [Res() for _ in range(2)]
                f_ss = ar.alloc([1], F32); f_ln = ar.alloc([1], F32); f_rs = ar.alloc([1], F32)
                r_fs = Res()
                MSET(S.pool, gT, 0.0, r_gT)
                wcnt = [0]

                def wload(ci):
                    k = wcnt[0] % 3
                    wcnt[0] += 1
                    S.dma(S.sp, [(WR[k], wscr[ci])], d_w[k], reads=[r_wscr], writes=[r_WR[k]])
                    return k

                pbk = [1, 2, 3, 4]
                ycl = [0]
                for bi, (w0, nt, tok0) in enumerate(blocks):
                    N = nt * 128
                    last = bi == len(blocks) - 1
                    c1 = 1 + w0
                    def load_x(w0_, nt_):
                        for j in range(nt_):
                            ta, tb = w0_ + 128 * j, w0_ + 128 * (j + 1)
                            lo, hi = max(ta, 0), min(tb, TK)
                            if lo > ta or hi < tb:
                                MSET(S.pool, xin[:, j, :], 0.0, [r_xin[j]])
                            S.dma(S.sp, [(xin[lo - ta:hi - ta, j, :], gx[lo:hi, :])], d_xt[j], writes=[r_xin[j]])
                    if bi == 0:
                        load_x(w0, nt)

                    def proj_residual(ci0, lhs_fn, rl, from_xin=False, norm=True, rl_fn=None):
                        ks = [wload(ci0), wload(ci0 + 1)]
                        for j in range(nt):
                            for ch in range(2):
                                k = ks[ch]
                                w3 = WR[k].rearrange("p (k n) -> p k n", k=8)
                                bk = pbk[(2 * j + ch) % 4]
                                for kc in range(8):
                                    MM(ps[:, bk, :], lhs_fn(kc, j), w3[:, kc, :], kc == 0, kc == 7, (rl_fn(kc) if rl_fn else rl) + [r_WR[k]], [rps[bk]], inc=(kc == 7))
                                src, rsrc = (xin, r_xin[j]) if from_xin else (xt, r_xt[j])
                                TT(S.dve, xt[:, j, ch * 512:(ch + 1) * 512], src[:, j, ch * 512:(ch + 1) * 512], ps[:, bk, :], ALU.add,
                                   [rps[bk], rsrc, r_xt[j]], [r_xt[j]])
                            if norm and j >= 1:
                                rms_T(xt[:, j - 1, :], r_xt[j - 1], hT, r_hT, 128 * (j - 1), 0, S.dve)
                        if norm:
                            rms_T(xt[:, nt - 1, :], r_xt[nt - 1], hT, r_hT, 128 * (nt - 1), 0, S.dve)

                    proj_residual(0, lambda kc, j: (AT[:, kc, c1 + 128 * j:c1 + 128 * (j + 1)] if kc < 4
                                                    else BT[:, kc - 4, c1 + 128 * j:c1 + 128 * (j + 1)]), [r_AT, r_BT], from_xin=True)
                    if bi + 1 < len(blocks):
                        load_x(blocks[bi + 1][0], blocks[bi + 1][1])
                    wkq = {}

                    def x_qa(hm):
                        cq, hh = hm // 2, hm % 2
                        if hh == 0:
                            wkq[cq] = wload(2 + cq)
                        k = wkq[cq]
                        w3 = WR[k].rearrange("p (k n) -> p k n", k=8)
                        qb = qm2[hm % 2]
                        for dc in range(2):
                            bk = pbk[dc]
                            for kc in range(8):
                                MM(ps[:, bk, 0:N], w3[:, kc, (2 * hh + dc) * 128:(2 * hh + dc + 1) * 128], hT[:, kc, 0:N], kc == 0, kc == 7,
                                   [r_WR[k], r_hT], [rps[bk]], inc=(kc == 7))
                            CP(S.dve, qb[:, dc, 0:N], ps[:, bk, 0:N], [rps[bk]], [r_qm2[hm % 2]])

                    def x_sb(hm):
                        qb = qm2[hm % 2]
                        for kc2 in range(2):
                            bk = pbk[2 + kc2]
                            for dc in range(2):
                                MM(ps[:, bk, 0:N], KmT[:, 2 * hm + dc, kc2 * 128:(kc2 + 1) * 128], qb[:, dc, 0:N], dc == 0, dc == 1,
                                   [r_Km, r_qm2[hm % 2]], [rps[bk]], inc=(dc == 1))
                            ACT(Em[kc2][:, 0:N], ps[:, bk, 0:N], AF.Exp, [rps[bk]], [r_Em[kc2]], scale=1.0 / 16)

                    def x_vc(hm):
                        for kc2 in range(2):
                            MM(ps[:, 7, 0:N], onesb, Em[kc2][:, 0:N], kc2 == 0, kc2 == 1, [r_const, r_Em[kc2]], [rps[7]], inc=(kc2 == 1))
                        ACT(rDm[:, 0:N], ps[:, 7, 0:N], AF.Ln, [rps[7]], [r_rDm])
                        ACT(rDm[:, 0:N], rDm[:, 0:N], AF.Exp, [r_rDm], [r_rDm], scale=-1.0)
                        for dc in range(2):
                            bk = 5 + dc
                            for kc2 in range(2):
                                MM(ps[:, bk, 0:N], Vm[:, kc2, hm * 256 + dc * 128:hm * 256 + (dc + 1) * 128], Em[kc2][:, 0:N], kc2 == 0, kc2 == 1,
                                   [r_Vm, r_Em[kc2]], [rps[bk]], inc=(kc2 == 1))
                            TT(S.dve, omT[:, 2 * hm + dc, 0:N], ps[:, bk, 0:N], rDm[:, 0:N], ALU.mult, [rps[bk], r_rDm], [r_om[2 * hm + dc]])

                    x_qa(0)
                    for hm in range(4):
                        x_sb(hm)
                        if hm + 1 < 4:
                            x_qa(hm + 1)
                        x_vc(hm)
                    proj_residual(4, lambda kc, j: omT[:, kc, 128 * j:128 * (j + 1)], None, rl_fn=lambda kc: [r_om[kc]])
                    if w0 < 0:
                        MSET(S.pool, hT[:, :, 0:1], 0.0, [r_hT])
                    if last and not g["halo"]:
                        MSET(S.pool, hT[:, :, N - 1:N], 0.0, [r_hT])
                    for cu in range(11):
                        k = wload(6 + cu)
                        w4 = WR[k].rearrange("p (a k n) -> p a k n", a=2, k=8)
                        for pp in range(2):
                            cf = 2 * cu + pp
                            bv_, bg_ = ((1, 2), (3, 4), (5, 6))[cf % 3]
                            accv, accg, tg = accv2[cf % 2], accg2[cf % 2], tg2[cf % 2]
                            r_av, r_ag, r_tg = r_av2[cf % 2], r_ag2[cf % 2], r_tg2[cf % 2]
                            for kc in range(8):
                                MM(ps[:, bv_, 0:N], w4[:, pp, kc, 0:128], hT[:, kc, 0:N], kc == 0, kc == 7, [r_WR[k], r_hT], [rps[bv_]], inc=(kc == 7))
                            for kc in range(8):
                                MM(ps[:, bg_, 0:N], w4[:, pp, kc, 128:256], hT[:, kc, 0:N], kc == 0, kc == 7, [r_WR[k], r_hT], [rps[bg_]], inc=(kc == 7))
                            for (acc, racc, bk, cc) in ((accv, r_av, bv_, cf), (accg, r_ag, bg_, 22 + cf)):
                                wc = C_FFNW + 3 * cc
                                ACT(acc[:, 1:N - 1], ps[:, bk, 1:N - 1], AF.Identity, [rps[bk], r_const], [racc], scale=col(wc + 1), bias=col(C_FFNB + cc))
                                STT(acc[:, 1:N - 1], ps[:, bk, 0:N - 2], col(wc), acc[:, 1:N - 1], ALU.mult, ALU.add, [rps[bk], racc, r_const], [racc])
                                STT(acc[:, 1:N - 1], ps[:, bk, 2:N], col(wc + 2), acc[:, 1:N - 1], ALU.mult, ALU.add, [rps[bk], racc, r_const], [racc])
                            ACT(tg[:, 1:N - 1], accg[:, 1:N - 1], AF.Tanh, [r_ag], [r_tg], scale=0.5)
                            STT(tg[:, 1:N - 1], tg[:, 1:N - 1], 1.0, accg[:, 1:N - 1], ALU.add, ALU.mult, [r_tg, r_ag], [r_tg])
                            TT(S.pool, gT[:, cf, 1:N - 1], tg[:, 1:N - 1], accv[:, 1:N - 1], ALU.mult, [r_tg, r_av], [r_gT[cf]])
                    def final_tile(j):
                        ky = ycl[0] % 2
                        ycl[0] += 1
                        ACT(junk, xt[:, j, :], AF.Square, [r_xt[j]], [r_fs], accum_out=f_ss)
                        ACT(f_ln, f_ss, AF.Ln, [r_fs, r_const], [r_fs], scale=1.0 / D, bias=epsb)
                        ACT(f_rs, f_ln, AF.Exp, [r_fs], [r_fs], scale=-0.5)
                        STT(xt[:, j, :], xt[:, j, :], f_rs, gfb, ALU.mult, ALU.mult, [r_xt[j], r_fs, r_const], [r_xt[j]])
                        ta = w0 + 128 * j
                        lo = max(tok0, ta + (1 if j == 0 else 0))
                        hi = min(TOWN, ta + 128 - (1 if j == nt - 1 else 0))
                        if hi > lo:
                            S.dma(S.pool, [(g["y"][lo:hi, :], xt[lo - ta:hi - ta, j, :])], d_out[ky], reads=[r_xt[j]], writes=[r_out])

                    for ch in range(2):
                        for kci in range(3):
                            k = wload(17 + 3 * ch + kci)
                            w3 = WR[k].rearrange("p (k n) -> p k n", k=8)
                            nk8 = 8 if kci < 2 else 6
                            for j in range(nt):
                                bk = pbk[j]
                                for kk in range(nk8):
                                    kc = 8 * kci + kk
                                    MM(ps[:, bk, :], gT[:, kc, 128 * j:128 * (j + 1)], w3[:, kk, :], kc == 0, kc == 21, [r_gT[kc], r_WR[k]], [rps[bk]],
                                       inc=(kk == nk8 - 1))
                                if kci == 2:
                                    TT(S.dve, xt[:, j, ch * 512:(ch + 1) * 512], xt[:, j, ch * 512:(ch + 1) * 512], ps[:, bk, :], ALU.add,
                                       [rps[bk], r_xt[j]], [r_xt[j]])
                                    if ch == 1 and j >= 1:
                                        final_tile(j - 1)
                    final_tile(nt - 1)
                barrier()
                checkpoint('phase3')


        except _Stop:
            barrier()
        S.final_wait(S.pool, [r_out])
        for e in S.engines:
            deps = [(d.sem, d.count) for d in d_out if d.count > 0]
            S._wait(e, deps)
        S.emit()
        build.stats = {e.name: (len(e.prog), e.nwaits) for e in S.engines}
        build.stats['arena_hw'] = ar.hw
    return nc


_NC_CACHE = {}


def _rope_table(pos):
    inv = (np.float32(10000.0) ** (-(np.arange(0, 64, 2, dtype=np.float32)) / np.float32(64))).astype(np.float32)
    ang = (pos.astype(np.float32)[:, None] * inv[None, :]).astype(np.float32)
    c, s = np.cos(ang).astype(np.float32), np.sin(ang).astype(np.float32)
    t = np.concatenate([c, s, -s], axis=1)
    return np.ascontiguousarray(t.reshape(-1, 128, 96))


def _cols(p, flip):
    f = lambda a: np.asarray(a, np.float32)
    cw = f(p["conv_dw_w"][0]); fw = f(p["ffn_dw_w"][0])
    if flip:
        cw = cw[::-1]; fw = fw[::-1]
    parts = [
        f(p["norm_mix"][0]).reshape(8, 128).T, f(p["norm_cross"][0]).reshape(8, 128).T,
        f(p["norm_mem"][0]).reshape(8, 128).T, f(p["norm_ffn"][0]).reshape(8, 128).T,
        f(p["conv_norm"][0]).reshape(4, 128).T, f(p["diff_subln"][0]).reshape(128, 1),
        f(p["conv_dw_b"][0]).reshape(4, 128).T,
        cw.reshape(31, 4, 128).transpose(2, 1, 0).reshape(128, 124),
        f(p["ffn_dw_b"][0]).reshape(44, 128).T,
        fw.reshape(3, 44, 128).transpose(2, 1, 0).reshape(128, 132),
    ]
    out = np.zeros((128, NCOL), np.float32)
    c = np.concatenate(parts, axis=1)
    out[:, :c.shape[1]] = c
    return out


def run(inputs, TP, TS, n_cores=8):
    p = inputs
    key = (TP, TS)
    if key not in _NC_CACHE:
        _NC_CACHE[key] = build(TP, TS)
    nc = _NC_CACHE[key]
    f = lambda a: np.ascontiguousarray(np.asarray(a, np.float32))
    xP, xS = f(p["x_prompt"]), f(p["x_sample"])
    mP, mS = f(p["mem_prompt"]), f(p["mem_sample"])
    TSO = TS // 2
    ident = np.eye(128, dtype=np.float32)
    lamv = np.stack([f(p["lambda_q1"][0]), f(p["lambda_k1"][0]), f(p["lambda_q2"][0]), f(p["lambda_k2"][0])])
    shared = dict(ident=ident, lamv=lamv, gfin=f(p["norm_final"]), w_in=f(p["w_in"][0]), w_out=f(p["w_out"][0]),
                  w_mq=f(p["w_mq"][0]), w_mkv=f(p["w_mkv"][0]), w_mo=f(p["w_mo"][0]), w_up=f(p["w_up"][0]), w_down=f(p["w_down"][0]))
    colsv = [_cols(p, False), _cols(p, True)]
    posP = np.arange(TP); posS = np.arange(TS)
    tabs = {0: (_rope_table(posP), _rope_table(posS)), 1: (_rope_table(posP[::-1]), _rope_table(posS[::-1]))}
    in_maps = []
    for c in range(n_cores):
        rev = c % 2
        xp_ = xP[c][::-1] if rev else xP[c]
        xs_ = xS[c // 2][::-1] if rev else xS[c // 2]
        m = dict(shared)
        m.update(xp=np.ascontiguousarray(xp_), xs=np.ascontiguousarray(xs_), memp=mP[c], mems=mS[c // 2],
                 cstp=tabs[rev][0], csts=tabs[rev][1], cols=colsv[rev])
        in_maps.append(m)
    res = run_bass_kernel_spmd(nc, in_maps, core_ids=list(range(n_cores)))
    yP = np.empty((n_cores, TP, D), np.float32)
    yS = np.empty((n_cores // 2, TS, D), np.float32)
    for c in range(n_cores):
        r = res.results[c]
        rev = c % 2
        yP[c] = r["yp"][::-1] if rev else r["yp"]
        if rev:
            yS[c // 2, TSO:] = r["ys"][::-1]
        else:
            yS[c // 2, :TSO] = r["ys"]
    return yP, yS


def kernel(**inputs):
    return run(inputs, 4096, 8192)
```
